# Optimizing a Trainium2 kernel written in Bass

```python
import jax, jax.numpy as jnp
from jax import lax
import numpy as np

D_MODEL = 1024
BATCH = 8
SEQ = 8192
DEPTH = 2

MIX_WIDTH = D_MODEL
ATTN_WIDTH = MIX_WIDTH // 2
POOL_WIDTH = MIX_WIDTH - ATTN_WIDTH
HEAD_DIM = 64
N_HEADS = ATTN_WIDTH // HEAD_DIM
N_KV_HEADS = 2
GROUP = N_HEADS // N_KV_HEADS
KV_WIDTH = N_KV_HEADS * HEAD_DIM
WINDOW = 128
BLOCK = 128
ROT_DIM = HEAD_DIM // 4
ROPE_THETA = 500000.0
POOL_WINDOWS = (2, 4, 8, 16)
N_POOL_GROUPS = len(POOL_WINDOWS)
POOL_GROUP_WIDTH = POOL_WIDTH // N_POOL_GROUPS
IN_WIDTH = ATTN_WIDTH + 2 * KV_WIDTH + POOL_WIDTH
D_FF = ((int(np.ceil(8 * D_MODEL / 3)) + 255) // 256) * 256
N_MOD = 6
EPS = 1e-6
NEG_INF = -1e30

kernel_name = "hybrid_swa_sink_pool_swiglu_block"


def rms_norm(x, g):
    xf = x.astype(jnp.float32)
    y = xf * lax.rsqrt(jnp.mean(xf * xf, axis=-1, keepdims=True) + EPS)
    return (y * g.astype(jnp.float32)).astype(x.dtype)


def partial_rotary(t, positions):
    inv_freq = ROPE_THETA ** (-jnp.arange(0, ROT_DIM, 2, dtype=jnp.float32) / ROT_DIM)
    ang = positions.astype(jnp.float32)[:, :, None] * inv_freq
    cos = jnp.cos(ang)[:, :, None, :]
    sin = jnp.sin(ang)[:, :, None, :]
    tf = t.astype(jnp.float32)
    half = ROT_DIM // 2
    t1, t2, rest = tf[..., :half], tf[..., half:ROT_DIM], tf[..., ROT_DIM:]
    rot = jnp.concatenate([t1 * cos - t2 * sin, t2 * cos + t1 * sin, rest], axis=-1)
    return rot.astype(t.dtype)


def sliding_window_attention_with_sinks(q, k, v, sinks):
    B, S = q.shape[0], q.shape[1]
    nb = S // BLOCK
    qb = q.reshape(B, nb, BLOCK, N_KV_HEADS, GROUP, HEAD_DIM)
    kb = k.reshape(B, nb, BLOCK, N_KV_HEADS, HEAD_DIM)
    vb = v.reshape(B, nb, BLOCK, N_KV_HEADS, HEAD_DIM)
    pad = ((0, 0), (1, 0), (0, 0), (0, 0), (0, 0))
    k_cat = jnp.concatenate([jnp.pad(kb, pad)[:, :-1], kb], axis=2)
    v_cat = jnp.concatenate([jnp.pad(vb, pad)[:, :-1], vb], axis=2)
    scores = jnp.einsum("bnqkgd,bnskd->bnkgqs", qb, k_cat).astype(jnp.float32)
    scores = scores * (HEAD_DIM ** -0.5)
    qi = jnp.arange(BLOCK)[:, None]
    kj = jnp.arange(2 * BLOCK)[None, :]
    diff = qi + BLOCK - kj
    blk = jnp.arange(nb)[:, None, None]
    key_abs = blk * BLOCK + kj[None] - BLOCK
    valid = (diff[None] >= 0) & (diff[None] < WINDOW) & (key_abs >= 0)
    scores = jnp.where(valid[None, :, None, None], scores, NEG_INF)
    sink = jnp.broadcast_to(
        sinks.astype(jnp.float32).reshape(1, 1, N_KV_HEADS, GROUP, 1, 1),
        scores.shape[:-1] + (1,))
    probs = jax.nn.softmax(jnp.concatenate([scores, sink], axis=-1), axis=-1)[..., :-1]
    out = jnp.einsum("bnkgqs,bnskd->bnqkgd", probs.astype(v.dtype), v_cat)
    return out.reshape(B, S, N_HEADS * HEAD_DIM)


def causal_pool_mixer(u, pool_w, pool_scale):
    S = u.shape[1]
    t = jnp.arange(S, dtype=jnp.float32)[None, :, None]
    outs = []
    for gi, w in enumerate(POOL_WINDOWS):
        ug = u[..., gi * POOL_GROUP_WIDTH:(gi + 1) * POOL_GROUP_WIDTH].astype(jnp.float32)
        cs = jnp.pad(jnp.cumsum(ug, axis=1), ((0, 0), (1, 0), (0, 0)))
        upper = cs[:, 1:]
        lower = jnp.pad(cs, ((0, 0), (w - 1, 0), (0, 0)))[:, :S]
        count = jnp.minimum(t + 1.0, float(w))
        pooled = (upper - lower) / count - ug
        outs.append(jnp.einsum("bsc,cd->bsd", pooled.astype(u.dtype), pool_w[gi]))
    return jnp.concatenate(outs, axis=-1) * pool_scale


def setup_inputs(seed: int = 0) -> dict:
    key = jax.random.key(seed)
    ks = jax.random.split(key, 20)
    f32 = jnp.float32
    def nrm(k, shape, scale):
        return jax.random.normal(k, shape, f32) * scale
    x = jax.random.normal(ks[0], (BATCH, SEQ, D_MODEL), f32)
    c = jax.random.normal(ks[1], (BATCH, D_MODEL), f32)
    offsets = jax.random.randint(ks[2], (BATCH, 1), 0, 4096, dtype=jnp.int32)
    positions = (offsets + jnp.arange(SEQ, dtype=jnp.int32)[None, :]).astype(jnp.int32)
    return {
        "x": x,
        "c": c,
        "positions": positions,
        "ada_w": nrm(ks[3], (DEPTH, D_MODEL, N_MOD * D_MODEL), D_MODEL ** -0.5),
        "ada_b": nrm(ks[4], (DEPTH, N_MOD * D_MODEL), 0.02),
        "w_in": nrm(ks[5], (DEPTH, D_MODEL, IN_WIDTH), D_MODEL ** -0.5),
        "b_in": nrm(ks[6], (DEPTH, IN_WIDTH), 0.02),
        "sinks": nrm(ks[7], (DEPTH, N_HEADS), 1.0),
        "pool_w": nrm(ks[8], (DEPTH, N_POOL_GROUPS, POOL_GROUP_WIDTH, POOL_GROUP_WIDTH), POOL_GROUP_WIDTH ** -0.5),
        "pool_scale": 1.0 + nrm(ks[9], (DEPTH, POOL_WIDTH), 0.1),
        "w_out": nrm(ks[10], (DEPTH, MIX_WIDTH, D_MODEL), MIX_WIDTH ** -0.5),
        "w_gate": nrm(ks[11], (DEPTH, D_MODEL, D_FF), D_MODEL ** -0.5),
        "w_up": nrm(ks[12], (DEPTH, D_MODEL, D_FF), D_MODEL ** -0.5),
        "w_down": nrm(ks[13], (DEPTH, D_FF, D_MODEL), D_FF ** -0.5),
        "g_pre_mix": 1.0 + nrm(ks[14], (DEPTH, D_MODEL), 0.02),
        "g_post_mix": 1.0 + nrm(ks[15], (DEPTH, D_MODEL), 0.02),
        "g_pre_ffn": 1.0 + nrm(ks[16], (DEPTH, D_MODEL), 0.02),
        "g_post_ffn": 1.0 + nrm(ks[17], (DEPTH, D_MODEL), 0.02),
    }


def reference(x, c, positions, ada_w, ada_b, w_in, b_in, sinks, pool_w, pool_scale,
              w_out, w_gate, w_up, w_down, g_pre_mix, g_post_mix, g_pre_ffn, g_post_ffn):
    B, S = x.shape[0], x.shape[1]
    c_act = jax.nn.silu(c)
    for l in range(DEPTH):
        mod = c_act @ ada_w[l] + ada_b[l]
        shift_m, scale_m, gate_m, shift_f, scale_f, gate_f = [
            m[:, None, :] for m in jnp.split(mod, N_MOD, axis=-1)]

        h = rms_norm(x, g_pre_mix[l]) * (1.0 + scale_m) + shift_m
        proj = h @ w_in[l] + b_in[l]
        q, k, v, u = jnp.split(
            proj, [ATTN_WIDTH, ATTN_WIDTH + KV_WIDTH, ATTN_WIDTH + 2 * KV_WIDTH], axis=-1)
        q = partial_rotary(q.reshape(B, S, N_HEADS, HEAD_DIM), positions)
        k = partial_rotary(k.reshape(B, S, N_KV_HEADS, HEAD_DIM), positions)
        v = v.reshape(B, S, N_KV_HEADS, HEAD_DIM)
        attn_out = sliding_window_attention_with_sinks(q, k, v, sinks[l])
        pool_out = causal_pool_mixer(u, pool_w[l], pool_scale[l])
        mix = jnp.concatenate([attn_out, pool_out], axis=-1) @ w_out[l]
        x = x + gate_m * rms_norm(mix, g_post_mix[l])

        h = rms_norm(x, g_pre_ffn[l]) * (1.0 + scale_f) + shift_f
        f = (jax.nn.silu(h @ w_gate[l]) * (h @ w_up[l])) @ w_down[l]
        x = x + gate_f * rms_norm(f, g_post_ffn[l])
    return x
```

```python
import numpy as np
from contextlib import ExitStack
import concourse.bass as bass
import concourse.mybir as mybir
from concourse.bass_utils import run_bass_kernel_spmd

F32 = mybir.dt.float32
BF16 = mybir.dt.bfloat16
I32 = mybir.dt.int32
ALU = mybir.AluOpType
AF = mybir.ActivationFunctionType

D = 1024
T = 512
DFF = 2816
NF = 22
NU = 63
VL = 45
NSLOT = 8
NWD = 2
EPS = 1e-6
MAGIC = 12582912.0
TWO_PI_SAFE = 6.28318
ROPE_THETA = 500000.0

U_K, U_V, U_U0, U_POOL, U_WO0, U_G0 = 4, 5, 6, 10, 11, 19


class Buf:
    __slots__ = ("w", "r", "const", "name")

    def __init__(self, name="", const=False):
        self.w = None
        self.r = {}
        self.const = const
        self.name = name


class DmaSem:
    def __init__(self, h):
        self.h = h
        self.count = 0


class Stream:
    def __init__(self, name, sem):
        self.name = name
        self.sem = sem
        self.count = 0
        self.ops = []
        self.seen = {}


class Tracker:
    def __init__(self):
        self.streams = {}

    def op(self, en, fn, reads=(), writes=(), signal=True, dma=None):
        st = self.streams[en]
        deps = []
        for b in reads:
            if b.w is not None:
                deps.append(b.w)
        for b in writes:
            if b.w is not None:
                deps.append(b.w)
            deps.extend(b.r.values())
        waits = {}
        for (sem, val) in deps:
            if en == "pe" and sem is st.sem:
                continue
            key = id(sem)
            if st.seen.get(key, 0) >= val:
                continue
            if key not in waits or waits[key][1] < val:
                waits[key] = (sem, val)
        for sem, val in waits.values():
            st.seen[id(sem)] = val
            st.ops.append(lambda e, sem=sem, val=val: e.wait_ge(sem, val))
        if dma is not None:
            dma.count += 16
            ev = (dma.h, dma.count)
            st.ops.append(lambda e, fn=fn, s=dma.h: fn(e).then_inc(s, 16))
        elif signal:
            st.count += 1
            ev = (st.sem, st.count)
            st.ops.append(lambda e, fn=fn, s=st.sem: fn(e).then_inc(s, 1))
        else:
            ev = (st.sem, st.count + 1)
            st.ops.append(lambda e, fn=fn: fn(e))
        for b in writes:
            b.w = ev
            b.r = {}
        for b in reads:
            if not b.const:
                k = id(ev[0])
                if k not in b.r or b.r[k][1] < ev[1]:
                    b.r[k] = ev
        return ev


def build_program(S, L=2, debug=False):
    NT = S // T
    nc = bass.Bass("TRN2", target_bir_lowering=False)

    def din(name, shape, dt=F32):
        return nc.dram_tensor(name, shape, dt, kind="ExternalInput").ap()

    xT_d = din("xT", [NT, 128, 8 * T])
    pos_d = din("pos", [128, S], I32)
    cT_d = din("cT", [128, 8])
    adaw_d = din("adaw", [L * 24, 128, 2048])
    adab_d = din("adab", [128, L * 48])
    wu_d = din("wu", [L * NU * 128 * 1024 // 16384, 16384])
    wd_d = din("wd", [L * 8 * 32, 11264])
    vec_d = din("vecs", [128, L * VL])
    bv_d = din("bvrep", [128, L * 512])
    snk_d = din("sinkrep", [128, L * 8])
    cst_d = din("cst", [128, 8])
    invc_d = din("invc", [128, 64])
    cmat_d = din("cmat", [128, 576])
    out_d = nc.dram_tensor("outT", [NT, 128, 8 * T], F32, kind="ExternalOutput").ap()
    dbg_d = nc.dram_tensor("dbg", [24, 128, 4096], F32, kind="ExternalOutput").ap() if debug else None
    dbgh_d = nc.dram_tensor("dbgh", [24, 128, 4096], BF16, kind="ExternalOutput").ap() if debug else None
    wub_d = nc.dram_tensor("wub", [L * NU * 128 * 1024 // 16384, 16384], BF16, kind="Internal").ap()
    wdb_d = nc.dram_tensor("wdb", [L * 8 * 32, 11264], BF16, kind="Internal").ap()

    K = Tracker()
    with ExitStack() as es:
        def sb(name, cols, dt=F32):
            return es.enter_context(nc.sbuf_tensor(name, [128, cols], dt))

        def sem(name):
            return es.enter_context(nc.semaphore(name))

        XT = [sb(f"xt{i}", 8 * T) for i in range(2)]
        XTB = [[Buf(f"x{i}_{c}") for c in range(8)] for i in range(2)]
        H = sb("h", 8 * T, BF16)
        HB = [Buf(f"h{c}") for c in range(8)]
        SQ = [sb(f"sq{i}", T, BF16) for i in range(2)]
        SQB = [Buf() for _ in range(2)]
        RS = sb("rs", T)
        RSB = Buf("rs")
        TMP = [sb(f"tmp{i}", T) for i in range(2)]
        TMPB = [Buf() for _ in range(2)]
        QB = [sb(f"qb{i}", T, BF16) for i in range(2)]
        QBB = [Buf() for _ in range(2)]
        QC = [sb(f"qc{i}", T, BF16) for i in range(2)]
        QCB = [Buf() for _ in range(2)]
        TR = [sb(f"tr{i}", T, BF16) for i in range(2)]
        TRB = [Buf() for _ in range(2)]
        QT = sb("qt", 4 * T, BF16)
        QTB = [Buf(f"qt{g}") for g in range(4)]
        KT = [sb(f"kt{l}", 2 * 640, BF16) for l in range(L)]
        KTB = [Buf(f"kt{l}") for l in range(L)]
        VT = [sb(f"vt{l}", 640, BF16) for l in range(L)]
        VTB = [Buf(f"vt{l}") for l in range(L)]
        UT = sb("ut", 4 * 528)
        UTB = [Buf(f"ut{g}") for g in range(4)]
        UH = [sb(f"uh{l}", 64) for l in range(L)]
        UHB = [Buf(f"uh{l}") for l in range(L)]
        LV = [sb(f"lv{i}", 528) for i in range(2)]
        LVB = [Buf() for _ in range(2)]
        T16 = sb("t16", 16)
        T16B = Buf()
        PL = sb("pl", 4 * T, BF16)
        PLB = [Buf() for _ in range(4)]
        PO = sb("po", 4 * T, BF16)
        POB = [Buf() for _ in range(4)]
        AT = sb("at", 4 * T, BF16)
        ATB = [Buf() for _ in range(4)]
        PR = [sb(f"pr{i}", T, BF16) for i in range(2)]
        PRB = [Buf() for _ in range(2)]
        PM = [sb(f"pm{i}", T, BF16) for i in range(8)]
        PMB = [Buf() for _ in range(8)]
        RT = sb("rt", T)
        RTB = Buf()
        MIXF = sb("mixf", 8 * T)
        MIXB = [Buf(f"mix{c}") for c in range(8)]
        ACTB = sb("actb", NF * T, BF16)
        ACTBB = [Buf() for _ in range(NF)]
        SG = [sb(f"sg{i}", T, BF16) for i in range(2)]
        SGB = [Buf() for _ in range(2)]
        CTAB = [sb(f"ctab{i}", T, BF16) for i in range(2)]
        STAB = [sb(f"stab{i}", T, BF16) for i in range(2)]
        CSB = [Buf() for _ in range(2)]
        POSI = sb("posi", T, I32)
        POSIB = Buf()
        MK = sb("mk", L * 4 * T, BF16)
        MKB = Buf(const=False)
        RING = [sb(f"ring{i}", 1024, BF16) for i in range(NSLOT)]
        RINGB = [Buf(f"ring{i}") for i in range(NSLOT)]
        WDR = [sb(f"wdr{i}", DFF, BF16) for i in range(NWD)]
        WDRB = [Buf(f"wdr{i}") for i in range(NWD)]
        CT = sb("ct", 8)
        CA = sb("ca", 8)
        CAB = Buf()
        ADAB = sb("adab_s", L * 48)
        MOD = sb("mod", L * 48)
        MODB = Buf()
        DER = sb("der", L * 32)
        DERB = Buf()
        VEC = sb("vec", L * VL)
        BVR = sb("bvr", L * 512)
        SNK = sb("snk", L * 8)
        ES = sb("es", L * 8)
        CST = sb("cst_s", 8)
        INVC = sb("invc_s", 64)
        CMATF = sb("cmatf", 576)
        CMAT = sb("cmat_s", 576, BF16)
        CONSTB = Buf("const")
        ONESM = CMAT[:, 0:128]
        ONES1 = CMAT[:, 128:192]
        PERM = CMAT[:, 192:320]

        PS = [es.enter_context(nc.psum_tensor(f"ps{i}", [128, 512], F32)) for i in range(8)]
        PSB = [Buf(f"ps{i}") for i in range(8)]

        s_pe, s_act, s_dve, s_pool, s_sp = sem("s_pe"), sem("s_act"), sem("s_dve"), sem("s_pool"), sem("s_sp")
        K.streams = {
            "pe": Stream("pe", s_pe), "act": Stream("act", s_act), "dve": Stream("dve", s_dve),
            "pool": Stream("pool", s_pool), "sp": Stream("sp", s_sp),
        }
        d_x = [DmaSem(sem(f"d_x{i}")) for i in range(2)]
        d_pos = DmaSem(sem("d_pos"))
        d_ring = [DmaSem(sem(f"d_ring{i}")) for i in range(NSLOT)]
        d_wd = [DmaSem(sem(f"d_wd{i}")) for i in range(NWD)]
        d_const = DmaSem(sem("d_const"))
        d_aw = [DmaSem(sem(f"d_aw{i}")) for i in range(2)]
        d_cast = DmaSem(sem("d_cast"))
        d_out = DmaSem(sem("d_out"))
        CASTB = Buf("cast")
        OUTB = Buf("outdram")

        bank_rr = [0]

        def bank():
            i = bank_rr[0] % 7
            bank_rr[0] += 1
            return PS[i], PSB[i]

        NRM, NRMB = PS[7], PSB[7]

        def mm(out, lhsT, rhs, start, stop, reads, writes, signal=None, tp=None):
            if signal is None:
                signal = stop
            if tp is None:
                fn = lambda e: e.matmul(out, lhsT, rhs, start=start, stop=stop)
            else:
                fn = lambda e: e.matmul(out, lhsT, rhs, start=start, stop=stop, tile_position=tp)
            K.op("pe", fn, reads, writes, signal)

        def dma(en, out, in_, reads, writes, dsem):
            K.op(en, lambda e: e.dma_start(out=out, in_=in_), reads, writes, dma=dsem)

        d_dbg = DmaSem(sem("d_dbg"))
        DBGB = Buf("dbg")

        def dump(idx, ap, cols, bufs, half=False):
            if debug:
                dst = dbgh_d if half else dbg_d
                dma("act", dst[idx, :, 0:cols], ap, list(bufs), [DBGB], d_dbg)

        nrow_u = L * NU * 128 * 1024 // 16384
        nrow_d = L * 8 * 32
        step = 63
        for r0 in range(0, nrow_u, step):
            r1 = min(nrow_u, r0 + step)
            dma("pool", wub_d[r0:r1, :], wu_d[r0:r1, :], [], [], d_cast)
        step = 64
        for r0 in range(0, nrow_d, step):
            r1 = min(nrow_d, r0 + step)
            dma("pool", wdb_d[r0:r1, :], wd_d[r0:r1, :], [], [], d_cast)
        CASTB.w = (d_cast.h, d_cast.count)

        for (dst, src) in ((CT, cT_d), (ADAB, adab_d), (VEC, vec_d), (BVR, bv_d), (SNK, snk_d), (CST, cst_d),
                           (INVC, invc_d), (CMATF, cmat_d)):
            dma("sp", dst[:, :], src[:, :], [], [CONSTB], d_const)
        K.op("dve", lambda e: e.tensor_copy(CMAT[:, :], CMATF[:, :]), [CONSTB], [CONSTB])
        CONSTB.const = False
        K.op("pool", lambda e: e.memset(UT[:, :], 0.0), [], UTB)
        for l in range(L):
            K.op("pool", lambda e, l=l: e.memset(KT[l][:, :], 0.0), [], [KTB[l]])
            K.op("pool", lambda e, l=l: e.memset(VT[l][:, :], 0.0), [], [VTB[l]])
            K.op("pool", lambda e, l=l: e.memset(UH[l][:, :], 0.0), [], [UHB[l]])
        for i in range(2):
            K.op("pool", lambda e, i=i: e.memset(LV[i][:, :], 0.0), [], [LVB[i]])
        K.op("act", lambda e: e.activation(CA[:, :], CT[:, :], AF.Silu), [CONSTB], [CAB])
        K.op("act", lambda e: e.activation(ES[:, :], SNK[:, :], AF.Exp, scale=-1.0), [CONSTB], [CONSTB])
        for l in range(L):
            for kv in range(2):
                for pc in range(2):
                    off = ((l * 2 + kv) * 2 + pc) * T
                    for g in range(4):
                        col = l * 8 + kv * 4 + g
                        K.op("dve", lambda e, off=off, g=g, pc=pc, col=col: e.tensor_scalar(
                            MK[:, off + g * 128: off + (g + 1) * 128], CMATF[:, 320 + pc * 128: 320 + (pc + 1) * 128],
                            ES[:, col:col + 1], 0.0, ALU.mult, ALU.add), [CONSTB], [MKB])
        modbank, modbankb = bank()
        for l in range(L):
            for pc in range(24):
                i = pc % 2
                stg = MIXF[:, i * 2048:(i + 1) * 2048]
                stgb = MIXB[i * 4:(i + 1) * 4]
                dma("sp", stg, adaw_d[l * 24 + pc, :, :], [], stgb, d_aw[i])
                for o in range(2):
                    col = l * 48 + pc * 2 + o
                    for kc in range(8):
                        mm(modbank[:, col:col + 1], stg[:, (o * 8 + kc) * 128:(o * 8 + kc + 1) * 128], CA[:, kc:kc + 1],
                           start=(kc == 0), stop=(kc == 7), reads=stgb + [CAB], writes=[modbankb])
        K.op("dve", lambda e: e.tensor_tensor(MOD[:, :], modbank[:, 0:L * 48], ADAB[:, :], ALU.add),
             [modbankb, CONSTB], [MODB])
        for l in range(L):
            m0 = l * 48
            v0 = l * VL
            d0 = l * 32
            K.op("dve", lambda e, m0=m0, v0=v0, d0=d0: e.scalar_tensor_tensor(
                DER[:, d0:d0 + 8], MOD[:, m0 + 8:m0 + 16], 1.0, VEC[:, v0:v0 + 8], ALU.add, ALU.mult), [MODB, CONSTB], [DERB])
            K.op("dve", lambda e, m0=m0, v0=v0, d0=d0: e.tensor_tensor(
                DER[:, d0 + 8:d0 + 16], MOD[:, m0 + 16:m0 + 24], VEC[:, v0 + 8:v0 + 16], ALU.mult), [MODB, CONSTB], [DERB])
            K.op("dve", lambda e, m0=m0, v0=v0, d0=d0: e.scalar_tensor_tensor(
                DER[:, d0 + 16:d0 + 24], MOD[:, m0 + 32:m0 + 40], 1.0, VEC[:, v0 + 16:v0 + 24], ALU.add, ALU.mult), [MODB, CONSTB], [DERB])
            K.op("dve", lambda e, m0=m0, v0=v0, d0=d0: e.tensor_tensor(
                DER[:, d0 + 24:d0 + 32], MOD[:, m0 + 40:m0 + 48], VEC[:, v0 + 24:v0 + 32], ALU.mult), [MODB, CONSTB], [DERB])
        BARRIER = [MKB, CONSTB, MODB, DERB]
        dump(0, MOD[:, :], L * 48, [MODB])
        dump(1, DER[:, :], L * 32, [DERB])
        dump(2, MK[:, :], L * 4 * T, [MKB], half=True)

        ucount = [0]
        wcount = [0]
        slot_of = {}
        wslot_of = {}

        def load_unit(t, l, u):
            s = ucount[0] % NSLOT
            ucount[0] += 1
            slot_of[(t, l, u)] = s
            r0 = (l * NU + u) * 8
            src = wub_d[r0:r0 + 8, :].rearrange("r (q c) -> (r q) c", q=16)
            dma("sp", RING[s][:, :], src, [CASTB], [RINGB[s]], d_ring[s])

        def load_wd(t, l, o):
            s = wcount[0] % NWD
            wcount[0] += 1
            wslot_of[(t, l, o)] = s
            r0 = (l * 8 + o) * 32
            src = wdb_d[r0:r0 + 32, :].rearrange("r (q c) -> (r q) c", c=DFF)
            dma("sp", WDR[s][:, :], src, [CASTB], [WDRB[s]], d_wd[s])

        ULIST = [(t_, l_, u_) for t_ in range(NT) for l_ in range(L) for u_ in range(NU)]
        WLIST = [(t_, l_, o_) for t_ in range(NT) for l_ in range(L) for o_ in range(8)]
        upos = [0]
        wpos = [0]

        def next_unit():
            if upos[0] < len(ULIST):
                load_unit(*ULIST[upos[0]])
                upos[0] += 1

        def next_wd():
            if wpos[0] < len(WLIST):
                load_wd(*WLIST[wpos[0]])
                wpos[0] += 1

        def W(t, l, u):
            s = slot_of[(t, l, u)]
            return RING[s], RINGB[s]

        def xc(X, c):
            return X[:, c * T:(c + 1) * T]

        def finish_rstd():
            K.op("act", lambda e: e.activation(RS[:, :], NRM[:, :], AF.Ln, bias=CST[:, 3:4], scale=1.0), [NRMB, CONSTB], [RSB])
            K.op("act", lambda e: e.activation(RS[:, :], RS[:, :], AF.Exp, scale=-0.5), [RSB], [RSB])

        def sumsq(src_ap, src_buf, idx):
            i = idx % 2
            K.op("pool", lambda e: e.tensor_tensor(SQ[i][:, :], src_ap, src_ap, ALU.mult), [src_buf], [SQB[i]])
            mm(NRM[:, :], ONESM, SQ[i][:, :], start=(idx == 0), stop=(idx == 7), reads=[SQB[i], CONSTB], writes=[NRMB], signal=True)

        def prenorm(X, XB, gs_col, sh_ap_col):
            for c in range(8):
                sumsq(xc(X, c), XB[c], c)
            finish_rstd()
            for c in range(8):
                i = c % 2
                K.op("dve", lambda e, c=c, i=i: e.tensor_tensor(TMP[i][:, :], xc(X, c), RS[:, :], ALU.mult), [XB[c], RSB], [TMPB[i]])
                K.op("act", lambda e, c=c, i=i: e.activation(xc(H, c), TMP[i][:, :], AF.Identity,
                                                                bias=MOD[:, sh_ap_col + c:sh_ap_col + c + 1],
                                                                scale=DER[:, gs_col + c:gs_col + c + 1]),
                     [TMPB[i]] + BARRIER, [HB[c]])

        def postnorm_residual(X, XB, gg_col):
            finish_rstd()
            for c in range(8):
                i = c % 2
                K.op("dve", lambda e, c=c, i=i: e.scalar_tensor_tensor(
                    TMP[i][:, :], xc(MIXF, c), DER[:, gg_col + c:gg_col + c + 1], RS[:, :], ALU.mult, ALU.mult),
                    [MIXB[c], RSB] + BARRIER, [TMPB[i]])
                K.op("pool", lambda e, c=c, i=i: e.tensor_tensor(xc(X, c), xc(X, c), TMP[i][:, :], ALU.add), [TMPB[i], XB[c]], [XB[c]])

        def proj_group(bk, bkb, unit, unitb, col0=0, ncols=T):
            for kc in range(8):
                mm(bk[:, col0:col0 + ncols], unit[:, kc * 128:(kc + 1) * 128], xc(H, kc), start=(kc == 0), stop=(kc == 7),
                   reads=[unitb, HB[kc]], writes=[bkb])
            next_unit()

        def make_tables(t):
            i = t % 2
            dma("sp", POSI[:, :], pos_d[:, t * T:(t + 1) * T], [], [POSIB], d_pos)
            POSF, TT_, KF, FR = TMP[0], TMP[1], RS, RT
            TABB = [TMPB[0], TMPB[1], RSB, RTB]
            K.op("dve", lambda e: e.tensor_copy(POSF[:, :], POSI[:, :]), [POSIB], TABB)
            for which in range(2):
                addc = 0.0 if which == 0 else 0.25
                K.op("dve", lambda e, addc=addc: e.tensor_scalar(TT_[:, :], POSF[:, :], CST[:, 0:1], addc, ALU.mult, ALU.add), TABB + [CONSTB], TABB)
                K.op("dve", lambda e: e.tensor_scalar(KF[:, :], TT_[:, :], MAGIC, MAGIC, ALU.add, ALU.subtract), TABB, TABB)
                K.op("dve", lambda e: e.tensor_tensor(FR[:, :], TT_[:, :], KF[:, :], ALU.subtract), TABB, TABB)
                if which == 0:
                    K.op("act", lambda e: e.activation(STAB[i][:, :], FR[:, :], AF.Sin, scale=CST[:, 1:2]), TABB + [CONSTB], [CSB[i]] + TABB)
                else:
                    K.op("act", lambda e: e.activation(CTAB[i][:, :], FR[:, :], AF.Sin, scale=TWO_PI_SAFE), TABB, [CSB[i]] + TABB)

        def tile_layer(t, l):
            first = (t == 0)
            X, XB = XT[t % 2], XTB[t % 2]
            ti = t % 2
            v0, d0, m0 = l * VL, l * 32, l * 48

            prenorm(X, XB, d0 + 0, m0 + 0)
            dbg = debug and t == 0 and l == 0
            if dbg:
                dump(3, H[:, :], 8 * T, HB, half=True)
            for g in range(5):
                u = g if g < 4 else U_K
                bcol = v0 + 32 + g
                i = g % 2
                bk, bkb = bank()
                unit, unitb = W(t, l, u)
                proj_group(bk, bkb, unit, unitb)
                K.op("act", lambda e, bk=bk, i=i, bcol=bcol: e.activation(QB[i][:, :], bk[:, :], AF.Identity, bias=VEC[:, bcol:bcol + 1], scale=1.0),
                     [bkb, CONSTB], [QBB[i]])
                b2, b2b = bank()
                mm(b2[:, :], PERM, QB[i][:, :], start=True, stop=True, reads=[QBB[i], CONSTB], writes=[b2b])
                K.op("dve", lambda e, b2=b2, i=i: e.tensor_tensor(TR[i][:, :], b2[:, :], STAB[ti][:, :], ALU.mult), [b2b, CSB[ti]], [TRB[i]])
                K.op("pool", lambda e, i=i: e.tensor_tensor(QC[i][:, :], QB[i][:, :], CTAB[ti][:, :], ALU.mult), [QBB[i], CSB[ti]], [QCB[i]])
                if g < 4:
                    K.op("pool", lambda e, i=i, g=g: e.tensor_tensor(xc(QT, g), QC[i][:, :], TR[i][:, :], ALU.add), [QCB[i], TRB[i]], [QTB[g]])
                else:
                    K.op("pool", lambda e, i=i: e.tensor_tensor(KT[l][0:64, 128:640], QC[i][0:64, :], TR[i][0:64, :], ALU.add),
                         [QCB[i], TRB[i]], [KTB[l]])
                    K.op("pool", lambda e, i=i: e.tensor_tensor(KT[l][64:128, 640 + 128:640 + 640], QC[i][64:128, :], TR[i][64:128, :], ALU.add),
                         [QCB[i], TRB[i]], [KTB[l]])
            bk, bkb = bank()
            unit, unitb = W(t, l, U_V)
            for blk in range(4):
                for kc in range(8):
                    mm(bk[:, blk * 128:(blk + 1) * 128], H[:, kc * T + blk * 128: kc * T + (blk + 1) * 128], unit[:, kc * 128:(kc + 1) * 128],
                       start=(kc == 0), stop=(kc == 7), reads=[unitb, HB[kc]], writes=[bkb], signal=(kc == 7 and blk == 3))
            next_unit()
            K.op("dve", lambda e, bk=bk: e.tensor_tensor(VT[l][:, 128:640], bk[:, :], BVR[:, l * 512:(l + 1) * 512], ALU.add), [bkb, CONSTB], [VTB[l]])
            for gi in range(4):
                bk, bkb = bank()
                unit, unitb = W(t, l, U_U0 + gi)
                proj_group(bk, bkb, unit, unitb)
                bcol = v0 + 37 + gi
                K.op("act", lambda e, bk=bk, gi=gi, bcol=bcol: e.activation(UT[:, gi * 528 + 16: gi * 528 + 528], bk[:, :], AF.Identity,
                                                                               bias=VEC[:, bcol:bcol + 1], scale=1.0), [bkb, CONSTB], [UTB[gi]])
            UT3 = UT[:, :].rearrange("p (g w) -> p g w", g=4)
            K.op("pool", lambda e: e.tensor_copy(UT3[:, :, 0:16], UH[l][:, :].rearrange("p (g w) -> p g w", g=4)), [UHB[l]], UTB)
            for gi in range(4):
                w = 2 << gi
                cur, curb = UT[:, gi * 528:(gi + 1) * 528], UTB[gi]
                for k in range(1, gi + 2):
                    sh = 1 << (k - 1)
                    dst, dstb = LV[k % 2], LVB[k % 2]
                    K.op("pool", lambda e, cur=cur, dst=dst, sh=sh: e.tensor_tensor(dst[:, sh:528], cur[:, sh:528], cur[:, 0:528 - sh], ALU.add),
                         [curb], [dstb])
                    cur, curb = dst[:, :], dstb
                K.op("dve", lambda e, cur=cur, gi=gi, w=w: e.scalar_tensor_tensor(
                    xc(PL, gi), cur[:, 16:528], 1.0 / w, UT[:, gi * 528 + 16:gi * 528 + 528], ALU.mult, ALU.subtract), [curb, UTB[gi]], [PLB[gi]])
                if first:
                    K.op("pool", lambda e, cur=cur, gi=gi: e.tensor_tensor(T16[:, :], cur[:, 16:32], INVC[:, gi * 16:(gi + 1) * 16], ALU.mult),
                         [curb, CONSTB], [T16B])
                    K.op("pool", lambda e, gi=gi: e.tensor_tensor(PL[:, gi * T:gi * T + 16], T16[:, :], UT[:, gi * 528 + 16:gi * 528 + 32], ALU.subtract),
                         [T16B, UTB[gi]], [PLB[gi]])
            K.op("pool", lambda e: e.tensor_copy(UH[l][:, :].rearrange("p (g w) -> p g w", g=4), UT3[:, :, 512:528]), UTB, [UHB[l]])
            punit, punitb = W(t, l, U_POOL)
            for gi in range(4):
                bk, bkb = bank()
                mm(bk[:, :], punit[:, gi * 128:(gi + 1) * 128], xc(PL, gi), start=True, stop=True, reads=[punitb, PLB[gi]], writes=[bkb])
                if gi == 3:
                    next_unit()
                scol = v0 + 41 + gi
                K.op("act", lambda e, bk=bk, gi=gi, scol=scol: e.activation(xc(PO, gi), bk[:, :], AF.Identity, scale=VEC[:, scol:scol + 1]),
                     [bkb, CONSTB], [POB[gi]])

            if dbg:
                dump(4, QT[:, :], 4 * T, QTB, half=True)
                dump(5, KT[l][:, :], 1280, [KTB[l]], half=True)
                dump(6, VT[l][:, :], 640, [VTB[l]], half=True)
                dump(7, UT[:, :], 4 * 528, UTB)
                dump(8, PL[:, :], 4 * T, PLB, half=True)
                dump(9, PO[:, :], 4 * T, POB, half=True)
                dump(15, CTAB[ti][:, :], T, [CSB[ti]], half=True)
                dump(16, STAB[ti][:, :], T, [CSB[ti]], half=True)
            pr_i = [0]

            def scores(j):
                for kv in range(2):
                    for pc in range(2):
                        if first and j == 0 and pc == 0:
                            continue
                        bk, bkb = bank()
                        kcol = kv * 640 + (j + pc) * 128
                        for g in range(4):
                            mm(bk[:, g * 128:(g + 1) * 128], KT[l][:, kcol:kcol + 128], QT[:, g * T + j * 128: g * T + (j + 1) * 128],
                               start=True, stop=True, reads=[KTB[l], QTB[g]], writes=[bkb], signal=(g == 3))
                        r = pr_i[0] % 2
                        pr_i[0] += 1
                        K.op("act", lambda e, bk=bk, r=r: e.activation(PR[r][:, :], bk[:, :], AF.Exp, scale=0.125), [bkb], [PRB[r]])
                        pm = (j % 2) * 4 + kv * 2 + pc
                        moff = ((l * 2 + kv) * 2 + pc) * T
                        K.op("dve", lambda e, r=r, pm=pm, moff=moff: e.tensor_tensor(PM[pm][:, :], PR[r][:, :], MK[:, moff:moff + T], ALU.mult),
                             [PRB[r]] + BARRIER, [PMB[pm]])

            def pv(j):
                pcs = [1] if (first and j == 0) else [0, 1]
                ob, obb = bank()
                db, dbb = bank()
                for pc in pcs:
                    for kv in range(2):
                        pm = (j % 2) * 4 + kv * 2 + pc
                        vcol = (j + pc) * 128 + kv * 64
                        last = (pc == pcs[-1] and kv == 1)
                        mm(ob[kv * 64:(kv + 1) * 64, :], VT[l][:, vcol:vcol + 64], PM[pm][:, :], start=(pc == pcs[0]), stop=(pc == pcs[-1]),
                           reads=[VTB[l], PMB[pm]], writes=[obb], signal=last, tp=(0, kv * 64))
                for pc in pcs:
                    for kv in range(2):
                        pm = (j % 2) * 4 + kv * 2 + pc
                        last = (pc == pcs[-1] and kv == 1)
                        mm(db[kv * 64:(kv + 1) * 64, :], ONES1, PM[pm][:, :], start=(pc == pcs[0]), stop=(pc == pcs[-1]),
                           reads=[CONSTB, PMB[pm]], writes=[dbb], signal=last, tp=(0, kv * 64))
                K.op("dve", lambda e, db=db: e.tensor_scalar(RT[:, :], db[:, :], 1.0, 0.0, ALU.add, ALU.add), [dbb], [RTB])
                K.op("dve", lambda e: e.reciprocal(RT[:, :], RT[:, :]), [RTB], [RTB])
                AT3 = AT[:, :].rearrange("p (g t) -> p g t", g=4)
                K.op("dve", lambda e, ob=ob, j=j: e.tensor_tensor(AT3[:, :, j * 128:(j + 1) * 128],
                                                                   ob[:, :].rearrange("p (g q) -> p g q", g=4),
                                                                   RT[:, :].rearrange("p (g q) -> p g q", g=4), ALU.mult), [obb, RTB], ATB)

            scores(0)
            for j in range(4):
                if j + 1 < 4:
                    scores(j + 1)
                pv(j)
            KT3 = KT[l][:, :].rearrange("p (m w) -> p m w", m=2)
            K.op("pool", lambda e: e.tensor_copy(KT3[:, :, 0:128], KT3[:, :, 512:640]), [KTB[l]], [KTB[l]])
            K.op("pool", lambda e: e.tensor_copy(VT[l][:, 0:128], VT[l][:, 512:640]), [VTB[l]], [VTB[l]])

            if dbg:
                dump(10, AT[:, :], 4 * T, ATB, half=True)
            for o in range(8):
                bk, bkb = bank()
                unit, unitb = W(t, l, U_WO0 + o)
                for kc in range(8):
                    rhs, rb = (xc(AT, kc), ATB[kc]) if kc < 4 else (xc(PO, kc - 4), POB[kc - 4])
                    mm(bk[:, :], unit[:, kc * 128:(kc + 1) * 128], rhs, start=(kc == 0), stop=(kc == 7), reads=[unitb, rb], writes=[bkb])
                next_unit()
                K.op("act", lambda e, bk=bk, o=o: e.activation(xc(MIXF, o), bk[:, :], AF.Identity), [bkb], [MIXB[o]])
                sumsq(xc(MIXF, o), MIXB[o], o)
            if dbg:
                dump(11, MIXF[:, :], 8 * T, MIXB)
            postnorm_residual(X, XB, d0 + 8)
            if dbg:
                dump(12, X[:, :], 8 * T, XB)

            prenorm(X, XB, d0 + 16, m0 + 24)
            for f in range(NF):
                bg, bgb = bank()
                unit, unitb = W(t, l, U_G0 + 2 * f)
                proj_group(bg, bgb, unit, unitb)
                bu, bub = bank()
                unit, unitb = W(t, l, U_G0 + 2 * f + 1)
                proj_group(bu, bub, unit, unitb)
                i = f % 2
                K.op("act", lambda e, bg=bg, i=i: e.activation(SG[i][:, :], bg[:, :], AF.Silu), [bgb], [SGB[i]])
                K.op("dve", lambda e, bu=bu, i=i, f=f: e.tensor_tensor(xc(ACTB, f), bu[:, :], SG[i][:, :], ALU.mult), [bub, SGB[i]], [ACTBB[f]])
            if dbg:
                dump(13, ACTB[:, 0:8 * T], 8 * T, ACTBB[0:8], half=True)
            for o in range(8):
                bk, bkb = bank()
                s = wslot_of[(t, l, o)]
                for f in range(NF):
                    mm(bk[:, :], WDR[s][:, f * 128:(f + 1) * 128], xc(ACTB, f), start=(f == 0), stop=(f == NF - 1),
                       reads=[WDRB[s], ACTBB[f]], writes=[bkb])
                next_wd()
                K.op("act", lambda e, bk=bk, o=o: e.activation(xc(MIXF, o), bk[:, :], AF.Identity), [bkb], [MIXB[o]])
                sumsq(xc(MIXF, o), MIXB[o], o)
            postnorm_residual(X, XB, d0 + 24)
            if dbg:
                dump(14, X[:, :], 8 * T, XB)

        dma("sp", XT[0][:, :], xT_d[0, :, :], [], XTB[0], d_x[0])
        for _ in range(NSLOT):
            next_unit()
        for _ in range(NWD):
            next_wd()
        for t in range(NT):
            if t + 1 < NT:
                i = (t + 1) % 2
                dma("sp", XT[i][:, :], xT_d[t + 1, :, :], [], XTB[i], d_x[i])
            make_tables(t)
            for l in range(L):
                tile_layer(t, l)
            dma("pool", out_d[t, :, :], XT[t % 2][:, :], XTB[t % 2], [OUTB], d_out)
        K.streams["pool"].ops.append(lambda e: e.wait_ge(d_out.h, d_out.count))
        if debug:
            K.streams["act"].ops.append(lambda e: e.wait_ge(d_dbg.h, d_dbg.count))

        with nc.Block() as block:
            @block.sync
            def _(e):
                for f in K.streams["sp"].ops:
                    f(e)

            @block.tensor
            def _(e):
                for f in K.streams["pe"].ops:
                    f(e)

            @block.scalar
            def _(e):
                for f in K.streams["act"].ops:
                    f(e)

            @block.vector
            def _(e):
                for f in K.streams["dve"].ops:
                    f(e)

            @block.gpsimd
            def _(e):
                for f in K.streams["pool"].ops:
                    f(e)
    return nc


def _unit(Wm, rows, cols):
    sub = Wm[np.asarray(rows)][:, np.asarray(cols)]
    return sub.reshape(8, 128, 128).transpose(1, 0, 2)


def pack_shared(inp, L=2):
    f32 = np.float32
    ada_w, ada_b = np.asarray(inp["ada_w"], f32), np.asarray(inp["ada_b"], f32)
    w_in, b_in = np.asarray(inp["w_in"], f32), np.asarray(inp["b_in"], f32)
    sinks = np.asarray(inp["sinks"], f32)
    pool_w, pool_scale = np.asarray(inp["pool_w"], f32), np.asarray(inp["pool_scale"], f32)
    w_out = np.asarray(inp["w_out"], f32)
    w_gate, w_up, w_down = np.asarray(inp["w_gate"], f32), np.asarray(inp["w_up"], f32), np.asarray(inp["w_down"], f32)
    gs = [np.asarray(inp[k], f32) for k in ("g_pre_mix", "g_post_mix", "g_pre_ffn", "g_post_ffn")]

    adaw = np.empty((L * 24, 128, 2, 8, 128), f32)
    for l in range(L):
        adaw[l * 24:(l + 1) * 24] = ada_w[l].reshape(8, 128, 24, 2, 128).transpose(2, 1, 3, 0, 4)
    adaw = adaw.reshape(L * 24, 128, 2048)
    adab = np.concatenate([ada_b[l].reshape(48, 128).T for l in range(L)], axis=1)

    nat = np.arange(1024)
    qcols = [np.concatenate([np.arange(g * 64, g * 64 + 64), np.arange((g + 4) * 64, (g + 4) * 64 + 64)]) for g in range(4)]
    mixrows = np.concatenate([qcols[g] for g in range(4)] + [np.arange(512, 1024)])
    wu = np.zeros((L, NU, 128, 8, 128), f32)
    wd = np.empty((L, 8, 128, NF, 128), f32)
    for l in range(L):
        for g in range(4):
            wu[l, g] = _unit(w_in[l], nat, qcols[g])
        wu[l, U_K] = _unit(w_in[l], nat, np.arange(512, 640))
        wu[l, U_V] = _unit(w_in[l], nat, np.arange(640, 768))
        for gi in range(4):
            wu[l, U_U0 + gi] = _unit(w_in[l], nat, np.arange(768 + gi * 128, 768 + (gi + 1) * 128))
            wu[l, U_POOL, :, gi, :] = pool_w[l, gi]
        for o in range(8):
            wu[l, U_WO0 + o] = _unit(w_out[l], mixrows, np.arange(o * 128, (o + 1) * 128))
            wd[l, o] = w_down[l][:, o * 128:(o + 1) * 128].reshape(NF, 128, 128).transpose(1, 0, 2)
        for f in range(NF):
            wu[l, U_G0 + 2 * f] = _unit(w_gate[l], nat, np.arange(f * 128, (f + 1) * 128))
            wu[l, U_G0 + 2 * f + 1] = _unit(w_up[l], nat, np.arange(f * 128, (f + 1) * 128))
    wu = wu.reshape(-1, 16384)
    wd = wd.reshape(-1, 11264)

    def pcol(v):
        return v.reshape(-1, 128).T

    vecs = np.empty((128, L * VL), f32)
    bvrep = np.empty((128, L * 512), f32)
    sinkrep = np.empty((128, L * 8), f32)
    for l in range(L):
        v0 = l * VL
        for i in range(4):
            vecs[:, v0 + 8 * i:v0 + 8 * i + 8] = pcol(gs[i][l])
        for g in range(4):
            vecs[:, v0 + 32 + g] = b_in[l][qcols[g]]
        vecs[:, v0 + 36] = b_in[l][512:640]
        vecs[:, v0 + 37:v0 + 41] = pcol(b_in[l][768:1280])
        vecs[:, v0 + 41:v0 + 45] = pcol(pool_scale[l])
        bvrep[:, l * 512:(l + 1) * 512] = np.tile(b_in[l][640:768], (128, 4))
        sinkrep[:, l * 8:(l + 1) * 8] = np.tile(sinks[l], (128, 1))

    cst = np.zeros((128, 8), f32)
    inv_freq = (ROPE_THETA ** (-np.arange(0, 16, 2, dtype=np.float32) / 16)).astype(f32)
    for p in range(128):
        r = p % 64
        if r < 16:
            cst[p, 0] = np.float32(inv_freq[r % 8]) / np.float32(2 * np.pi)
            cst[p, 1] = (-1.0 if r < 8 else 1.0) * TWO_PI_SAFE
    cst[:, 2] = TWO_PI_SAFE
    cst[:, 3] = EPS
    invc = np.zeros((128, 64), f32)
    for gi in range(4):
        w = 2 << gi
        invc[:, gi * 16:(gi + 1) * 16] = 1.0 / np.minimum(np.arange(16) + 1.0, float(w))
    cmat = np.zeros((128, 576), f32)
    cmat[:, 0:128] = 1.0 / 1024.0
    cmat[:, 128:192] = 1.0
    for m in range(128):
        r = m % 64
        if r < 8:
            cmat[m + 8, 192 + m] = 1.0
        elif r < 16:
            cmat[m - 8, 192 + m] = 1.0
    s_idx = np.arange(128)[:, None]
    q_idx = np.arange(128)[None, :]
    cmat[:, 320:448] = (s_idx > q_idx).astype(f32)
    cmat[:, 448:576] = (s_idx <= q_idx).astype(f32)
    return dict(adaw=adaw, adab=np.ascontiguousarray(adab), wu=wu, wd=wd, vecs=vecs, bvrep=bvrep, sinkrep=sinkrep,
                cst=cst, invc=invc, cmat=cmat)


def pack_core(x_b, c_b, pos_b):
    S = x_b.shape[0]
    NT = S // T
    xT = np.ascontiguousarray(np.asarray(x_b, np.float32).reshape(NT, T, 8, 128).transpose(0, 3, 2, 1)).reshape(NT, 128, 8 * T)
    pos = np.ascontiguousarray(np.broadcast_to(np.asarray(pos_b, np.int32)[None, :], (128, S)))
    cT = np.ascontiguousarray(np.asarray(c_b, np.float32).reshape(8, 128).T)
    return dict(xT=xT, pos=pos, cT=cT)


def unpack_out(outT, S):
    NT = S // T
    return outT.reshape(NT, 128, 8, T).transpose(0, 3, 2, 1).reshape(S, D)


_CACHE = {}


def kernel(_debug=False, **inputs):
    x = np.asarray(inputs["x"])
    B, S, _ = x.shape
    shared = pack_shared(inputs)
    in_maps = []
    for b in range(B):
        m = dict(shared)
        m.update(pack_core(x[b], np.asarray(inputs["c"])[b], np.asarray(inputs["positions"])[b]))
        in_maps.append(m)
    key = (S, _debug)
    if key not in _CACHE:
        _CACHE[key] = build_program(S, debug=_debug)
    nc = _CACHE[key]
    res = run_bass_kernel_spmd(nc, in_maps, core_ids=list(range(B)))
    if _debug:
        kernel.dbg = [np.where(np.arange(24)[:, None, None] >= 0, 0, 0) for b in range(0)]
        for b in range(B):
            d32 = np.asarray(res.results[b]["dbg"]).astype(np.float32)
            d16 = np.asarray(res.results[b]["dbgh"]).astype(np.float32)
            halfidx = [2, 3, 4, 5, 6, 8, 9, 15, 16, 10, 13]
            for i in halfidx:
                d32[i] = d16[i]
            kernel.dbg.append(d32)
    out = np.empty((B, S, D), np.float32)
    for b in range(B):
        out[b] = unpack_out(np.asarray(res.results[b]["outT"]), S)
    return out
```

```python
import numpy as np
from contextlib import ExitStack
import concourse.bass as bass
import concourse.mybir as mybir
from concourse.bass_utils import run_bass_kernel_spmd

F32 = mybir.dt.float32
BF16 = mybir.dt.bfloat16
I32 = mybir.dt.int32
ALU = mybir.AluOpType
AF = mybir.ActivationFunctionType

D = 1024
T = 512
DFF = 2816
NF = 22
NU = 63
VL = 45
NSLOT = 8
NWD = 3
EPS = 1e-6
MAGIC = 12582912.0
TWO_PI_SAFE = 6.28318
ROPE_THETA = 500000.0

U_K, U_V, U_U0, U_POOL, U_WO0, U_G0 = 4, 5, 6, 10, 11, 19


class Buf:
    __slots__ = ("w", "r", "const", "name")

    def __init__(self, name="", const=False):
        self.w = None
        self.r = {}
        self.const = const
        self.name = name


class DmaSem:
    def __init__(self, h):
        self.h = h
        self.count = 0


class Stream:
    def __init__(self, name, sem):
        self.name = name
        self.sem = sem
        self.count = 0
        self.ops = []
        self.seen = {}


class Tracker:
    def __init__(self):
        self.streams = {}

    def op(self, en, fn, reads=(), writes=(), signal=True, dma=None):
        st = self.streams[en]
        deps = []
        for b in reads:
            if b.w is not None:
                deps.append(b.w)
        for b in writes:
            if b.w is not None:
                deps.append(b.w)
            deps.extend(b.r.values())
        waits = {}
        for (sem, val) in deps:
            if en == "pe" and sem is st.sem:
                continue
            key = id(sem)
            if st.seen.get(key, 0) >= val:
                continue
            if key not in waits or waits[key][1] < val:
                waits[key] = (sem, val)
        for sem, val in waits.values():
            st.seen[id(sem)] = val
            st.ops.append(lambda e, sem=sem, val=val: e.wait_ge(sem, val))
        if dma is not None:
            dma.count += 16
            ev = (dma.h, dma.count)
            st.ops.append(lambda e, fn=fn, s=dma.h: fn(e).then_inc(s, 16))
        elif signal:
            st.count += 1
            ev = (st.sem, st.count)
            st.ops.append(lambda e, fn=fn, s=st.sem: fn(e).then_inc(s, 1))
        else:
            ev = (st.sem, st.count + 1)
            st.ops.append(lambda e, fn=fn: fn(e))
        for b in writes:
            b.w = ev
            b.r = {}
        for b in reads:
            if not b.const:
                k = id(ev[0])
                if k not in b.r or b.r[k][1] < ev[1]:
                    b.r[k] = ev
        return ev


def build_program(S, L=2, debug=False):
    NT = S // T
    nc = bass.Bass("TRN2", target_bir_lowering=False)

    def din(name, shape, dt=F32):
        return nc.dram_tensor(name, shape, dt, kind="ExternalInput").ap()

    xT_d = din("xT", [NT, 128, 8 * T])
    pos_d = din("pos", [128, S], I32)
    cT_d = din("cT", [128, 8])
    adaw_d = din("adaw", [L * 24, 128, 2048])
    adab_d = din("adab", [128, L * 48])
    wu_d = din("wu", [L * NU * 128 * 1024 // 16384, 16384])
    wd_d = din("wd", [L * 8 * 32, 11264])
    vec_d = din("vecs", [128, L * VL])
    bv_d = din("bvrep", [128, L * 512])
    snk_d = din("sinkrep", [128, L * 8])
    cst_d = din("cst", [128, 8])
    invc_d = din("invc", [128, 64])
    cmat_d = din("cmat", [128, 576])
    out_d = nc.dram_tensor("outT", [NT, 128, 8 * T], F32, kind="ExternalOutput").ap()
    dbg_d = nc.dram_tensor("dbg", [24, 128, 4096], F32, kind="ExternalOutput").ap() if debug else None
    dbgh_d = nc.dram_tensor("dbgh", [24, 128, 4096], BF16, kind="ExternalOutput").ap() if debug else None
    wub_d = nc.dram_tensor("wub", [L * NU * 128 * 1024 // 16384, 16384], BF16, kind="Internal").ap()
    wdb_d = nc.dram_tensor("wdb", [L * 8 * 32, 11264], BF16, kind="Internal").ap()

    K = Tracker()
    with ExitStack() as es:
        def sb(name, cols, dt=F32):
            return es.enter_context(nc.sbuf_tensor(name, [128, cols], dt))

        def sem(name):
            return es.enter_context(nc.semaphore(name))

        XT = [sb(f"xt{i}", 8 * T) for i in range(2)]
        XTB = [[Buf(f"x{i}_{c}") for c in range(8)] for i in range(2)]
        H = sb("h", 8 * T, BF16)
        HB = [Buf(f"h{c}") for c in range(8)]
        SQ = [sb(f"sq{i}", T, BF16) for i in range(4)]
        SQB = [Buf() for _ in range(4)]
        RS = sb("rs", T)
        RSB = Buf("rs")
        TMP = [sb(f"tmp{i}", T) for i in range(4)]
        TMPB = [Buf() for _ in range(4)]
        QB = [sb(f"qb{i}", T, BF16) for i in range(5)]
        QBB = [Buf() for _ in range(5)]
        QC = [sb(f"qc{i}", T, BF16) for i in range(2)]
        QCB = [Buf() for _ in range(2)]
        TR = [sb(f"tr{i}", T, BF16) for i in range(2)]
        TRB = [Buf() for _ in range(2)]
        QT = sb("qt", 4 * T, BF16)
        QTB = [Buf(f"qt{g}") for g in range(4)]
        KT = [sb(f"kt{l}", 2 * 640, BF16) for l in range(L)]
        KTB = [Buf(f"kt{l}") for l in range(L)]
        VT = [sb(f"vt{l}", 640, BF16) for l in range(L)]
        VTB = [Buf(f"vt{l}") for l in range(L)]
        UT = sb("ut", 4 * 528)
        UTB = [Buf(f"ut{g}") for g in range(4)]
        UH = [sb(f"uh{l}", 64) for l in range(L)]
        UHB = [Buf(f"uh{l}") for l in range(L)]
        LV = [sb(f"lv{i}", 528) for i in range(2)]
        LVB = [Buf() for _ in range(2)]
        T16 = sb("t16", 16)
        T16B = Buf()
        PL = sb("pl", 4 * T, BF16)
        PLB = [Buf() for _ in range(4)]
        PO = sb("po", 4 * T, BF16)
        POB = [Buf() for _ in range(4)]
        AT = sb("at", 4 * T, BF16)
        ATB = [Buf() for _ in range(4)]
        PR = [sb(f"pr{i}", T, BF16) for i in range(2)]
        PRB = [Buf() for _ in range(2)]
        PM = [sb(f"pm{i}", T, BF16) for i in range(8)]
        PMB = [Buf() for _ in range(8)]
        RT = sb("rt", T)
        RTB = Buf()
        MIXF = sb("mixf", 8 * T)
        MIXB = [Buf(f"mix{c}") for c in range(8)]
        ACTB = sb("actb", NF * T, BF16)
        ACTBB = [Buf() for _ in range(NF)]
        SG = [sb(f"sg{i}", T, BF16) for i in range(2)]
        SGB = [Buf() for _ in range(2)]
        CTAB = [sb(f"ctab{i}", T, BF16) for i in range(2)]
        STAB = [sb(f"stab{i}", T, BF16) for i in range(2)]
        CSB = [Buf() for _ in range(2)]
        POSI = sb("posi", T, I32)
        POSIB = Buf()
        MK = sb("mk", L * 4 * T, BF16)
        MKB = Buf(const=False)
        RING = [sb(f"ring{i}", 1024, BF16) for i in range(NSLOT)]
        RINGB = [Buf(f"ring{i}") for i in range(NSLOT)]
        WDR = [sb(f"wdr{i}", DFF, BF16) for i in range(NWD)]
        WDRB = [Buf(f"wdr{i}") for i in range(NWD)]
        CT = sb("ct", 8)
        CA = sb("ca", 8)
        CAB = Buf()
        ADAB = sb("adab_s", L * 48)
        MOD = sb("mod", L * 48)
        MODB = Buf()
        DER = sb("der", L * 32)
        DERB = Buf()
        VEC = sb("vec", L * VL)
        BVR = sb("bvr", L * 512)
        SNK = sb("snk", L * 8)
        ES = sb("es", L * 8)
        CST = sb("cst_s", 8)
        INVC = sb("invc_s", 64)
        CMATF = MIXF[:, 0:576]
        CMAT = sb("cmat_s", 576, BF16)
        CONSTB = Buf("const")
        ONESM = CMAT[:, 0:128]
        ONES1 = CMAT[:, 128:192]
        PERM = CMAT[:, 192:320]

        PS = [es.enter_context(nc.psum_tensor(f"ps{i}", [128, 512], F32)) for i in range(8)]
        PSB = [Buf(f"ps{i}") for i in range(8)]

        s_pe, s_act, s_dve, s_pool, s_sp = sem("s_pe"), sem("s_act"), sem("s_dve"), sem("s_pool"), sem("s_sp")
        K.streams = {
            "pe": Stream("pe", s_pe), "act": Stream("act", s_act), "dve": Stream("dve", s_dve),
            "pool": Stream("pool", s_pool), "sp": Stream("sp", s_sp),
        }
        d_x = [DmaSem(sem(f"d_x{i}")) for i in range(2)]
        d_pos = DmaSem(sem("d_pos"))
        d_ring = [DmaSem(sem(f"d_ring{i}")) for i in range(NSLOT)]
        d_wd = [DmaSem(sem(f"d_wd{i}")) for i in range(NWD)]
        d_const = DmaSem(sem("d_const"))
        d_aw = [DmaSem(sem(f"d_aw{i}")) for i in range(2)]
        d_cast = DmaSem(sem("d_cast"))
        d_out = DmaSem(sem("d_out"))
        CASTB = Buf("cast")
        OUTB = Buf("outdram")

        bank_rr = [0]

        def bank():
            i = bank_rr[0] % 7
            bank_rr[0] += 1
            return PS[i], PSB[i]

        NRM, NRMB = PS[7], PSB[7]

        def mm(out, lhsT, rhs, start, stop, reads, writes, signal=None, tp=None):
            if signal is None:
                signal = stop
            if tp is None:
                fn = lambda e: e.matmul(out, lhsT, rhs, start=start, stop=stop)
            else:
                fn = lambda e: e.matmul(out, lhsT, rhs, start=start, stop=stop, tile_position=tp)
            K.op("pe", fn, reads, writes, signal)

        def dma(en, out, in_, reads, writes, dsem):
            K.op(en, lambda e: e.dma_start(out=out, in_=in_), reads, writes, dma=dsem)

        d_dbg = DmaSem(sem("d_dbg"))
        DBGB = Buf("dbg")

        def dump(idx, ap, cols, bufs, half=False):
            if debug:
                dst = dbgh_d if half else dbg_d
                dma("act", dst[idx, :, 0:cols], ap, list(bufs), [DBGB], d_dbg)

        nrow_u = L * NU * 128 * 1024 // 16384
        nrow_d = L * 8 * 32
        step = 63
        for r0 in range(0, nrow_u, step):
            r1 = min(nrow_u, r0 + step)
            dma("pool", wub_d[r0:r1, :], wu_d[r0:r1, :], [], [], d_cast)
        step = 64
        for r0 in range(0, nrow_d, step):
            r1 = min(nrow_d, r0 + step)
            dma("pool", wdb_d[r0:r1, :], wd_d[r0:r1, :], [], [], d_cast)
        CASTB.w = (d_cast.h, d_cast.count)

        for (dst, src) in ((CT, cT_d), (ADAB, adab_d), (VEC, vec_d), (BVR, bv_d), (SNK, snk_d), (CST, cst_d),
                           (INVC, invc_d)):
            dma("sp", dst[:, :], src[:, :], [], [CONSTB], d_const)
        dma("sp", MIXF[:, 0:576], cmat_d[:, :], [], [CONSTB, MIXB[0], MIXB[1]], d_const)
        K.op("dve", lambda e: e.tensor_copy(CMAT[:, :], MIXF[:, 0:576]), [CONSTB, MIXB[0], MIXB[1]], [CONSTB])
        CONSTB.const = False
        K.op("pool", lambda e: e.memset(UT[:, :], 0.0), [], UTB)
        for l in range(L):
            K.op("pool", lambda e, l=l: e.memset(KT[l][:, :], 0.0), [], [KTB[l]])
            K.op("pool", lambda e, l=l: e.memset(VT[l][:, :], 0.0), [], [VTB[l]])
            K.op("pool", lambda e, l=l: e.memset(UH[l][:, :], 0.0), [], [UHB[l]])
        for i in range(2):
            K.op("pool", lambda e, i=i: e.memset(LV[i][:, :], 0.0), [], [LVB[i]])
        K.op("act", lambda e: e.activation(CA[:, :], CT[:, :], AF.Silu), [CONSTB], [CAB])
        K.op("act", lambda e: e.activation(ES[:, :], SNK[:, :], AF.Exp, scale=-1.0), [CONSTB], [CONSTB])
        for l in range(L):
            for kv in range(2):
                for pc in range(2):
                    off = ((l * 2 + kv) * 2 + pc) * T
                    for g in range(4):
                        col = l * 8 + kv * 4 + g
                        K.op("dve", lambda e, off=off, g=g, pc=pc, col=col: e.tensor_scalar(
                            MK[:, off + g * 128: off + (g + 1) * 128], MIXF[:, 320 + pc * 128: 320 + (pc + 1) * 128],
                            ES[:, col:col + 1], 0.0, ALU.mult, ALU.add), [CONSTB, MIXB[0], MIXB[1]], [MKB])
        modbank, modbankb = bank()
        for l in range(L):
            for pc in range(24):
                i = pc % 2
                stg = MIXF[:, i * 2048:(i + 1) * 2048]
                stgb = MIXB[i * 4:(i + 1) * 4]
                dma("sp", stg, adaw_d[l * 24 + pc, :, :], [], stgb, d_aw[i])
                for o in range(2):
                    col = l * 48 + pc * 2 + o
                    for kc in range(8):
                        mm(modbank[:, col:col + 1], stg[:, (o * 8 + kc) * 128:(o * 8 + kc + 1) * 128], CA[:, kc:kc + 1],
                           start=(kc == 0), stop=(kc == 7), reads=stgb + [CAB], writes=[modbankb])
        K.op("dve", lambda e: e.tensor_tensor(MOD[:, :], modbank[:, 0:L * 48], ADAB[:, :], ALU.add),
             [modbankb, CONSTB], [MODB])
        for l in range(L):
            m0 = l * 48
            v0 = l * VL
            d0 = l * 32
            K.op("dve", lambda e, m0=m0, v0=v0, d0=d0: e.scalar_tensor_tensor(
                DER[:, d0:d0 + 8], MOD[:, m0 + 8:m0 + 16], 1.0, VEC[:, v0:v0 + 8], ALU.add, ALU.mult), [MODB, CONSTB], [DERB])
            K.op("dve", lambda e, m0=m0, v0=v0, d0=d0: e.tensor_tensor(
                DER[:, d0 + 8:d0 + 16], MOD[:, m0 + 16:m0 + 24], VEC[:, v0 + 8:v0 + 16], ALU.mult), [MODB, CONSTB], [DERB])
            K.op("dve", lambda e, m0=m0, v0=v0, d0=d0: e.scalar_tensor_tensor(
                DER[:, d0 + 16:d0 + 24], MOD[:, m0 + 32:m0 + 40], 1.0, VEC[:, v0 + 16:v0 + 24], ALU.add, ALU.mult), [MODB, CONSTB], [DERB])
            K.op("dve", lambda e, m0=m0, v0=v0, d0=d0: e.tensor_tensor(
                DER[:, d0 + 24:d0 + 32], MOD[:, m0 + 40:m0 + 48], VEC[:, v0 + 24:v0 + 32], ALU.mult), [MODB, CONSTB], [DERB])
        BARRIER = [MKB, CONSTB, MODB, DERB]
        dump(0, MOD[:, :], L * 48, [MODB])
        dump(1, DER[:, :], L * 32, [DERB])
        dump(2, MK[:, :], L * 4 * T, [MKB], half=True)

        ucount = [0]
        wcount = [0]
        slot_of = {}
        wslot_of = {}

        def load_unit(t, l, u):
            s = ucount[0] % NSLOT
            ucount[0] += 1
            slot_of[(t, l, u)] = s
            r0 = (l * NU + u) * 8
            src = wub_d[r0:r0 + 8, :].rearrange("r (q c) -> (r q) c", q=16)
            dma("sp", RING[s][:, :], src, [CASTB], [RINGB[s]], d_ring[s])

        def load_wd(t, l, o):
            s = wcount[0] % NWD
            wcount[0] += 1
            wslot_of[(t, l, o)] = s
            r0 = (l * 8 + o) * 32
            src = wdb_d[r0:r0 + 32, :].rearrange("r (q c) -> (r q) c", c=DFF)
            dma("sp", WDR[s][:, :], src, [CASTB], [WDRB[s]], d_wd[s])

        ULIST = [(t_, l_, u_) for t_ in range(NT) for l_ in range(L) for u_ in range(NU)]
        WLIST = [(t_, l_, o_) for t_ in range(NT) for l_ in range(L) for o_ in range(8)]
        upos = [0]
        wpos = [0]

        def next_unit():
            if upos[0] < len(ULIST):
                load_unit(*ULIST[upos[0]])
                upos[0] += 1

        def next_wd():
            if wpos[0] < len(WLIST):
                load_wd(*WLIST[wpos[0]])
                wpos[0] += 1

        def W(t, l, u):
            s = slot_of[(t, l, u)]
            return RING[s], RINGB[s]

        def xc(X, c):
            return X[:, c * T:(c + 1) * T]

        def finish_rstd():
            K.op("act", lambda e: e.activation(RS[:, :], NRM[:, :], AF.Ln, bias=CST[:, 3:4], scale=1.0), [NRMB, CONSTB], [RSB])
            K.op("act", lambda e: e.activation(RS[:, :], RS[:, :], AF.Exp, scale=-0.5), [RSB], [RSB])

        def sq_from(src_ap, src_buf, idx):
            i = idx % 4
            K.op("act", lambda e: e.activation(SQ[i][:, :], src_ap, AF.Square), [src_buf], [SQB[i]])

        def sumsq_mm(idx):
            i = idx % 4
            mm(NRM[:, :], ONESM, SQ[i][:, :], start=(idx == 0), stop=(idx == 7), reads=[SQB[i], CONSTB], writes=[NRMB], signal=True)

        def prenorm(X, XB, gs_col, sh_col, squares_done):
            if not squares_done:
                for c in range(8):
                    sq_from(xc(X, c), XB[c], c)
                    sumsq_mm(c)
            finish_rstd()
            for c in range(8):
                i = c % 4
                if c % 2 == 0:
                    K.op("dve", lambda e, c=c, i=i: e.tensor_tensor(TMP[i][:, :], xc(X, c), RS[:, :], ALU.mult), [XB[c], RSB], [TMPB[i]])
                    K.op("act", lambda e, c=c, i=i: e.activation(xc(H, c), TMP[i][:, :], AF.Identity,
                                                                    bias=MOD[:, sh_col + c:sh_col + c + 1],
                                                                    scale=DER[:, gs_col + c:gs_col + c + 1]),
                         [TMPB[i]] + BARRIER, [HB[c]])
                else:
                    K.op("pool", lambda e, c=c, i=i: e.tensor_tensor(TMP[i][:, :], xc(X, c), RS[:, :], ALU.mult), [XB[c], RSB], [TMPB[i]])
                    K.op("dve", lambda e, c=c, i=i: e.tensor_scalar(xc(H, c), TMP[i][:, :], DER[:, gs_col + c:gs_col + c + 1],
                                                                     MOD[:, sh_col + c:sh_col + c + 1], ALU.mult, ALU.add),
                         [TMPB[i]] + BARRIER, [HB[c]])

        def postnorm_residual(X, XB, gg_col, next_sq):
            finish_rstd()
            for c in range(8):
                i = c % 4
                K.op("dve", lambda e, c=c, i=i: e.scalar_tensor_tensor(
                    TMP[i][:, :], xc(MIXF, c), DER[:, gg_col + c:gg_col + c + 1], RS[:, :], ALU.mult, ALU.mult),
                    [MIXB[c], RSB] + BARRIER, [TMPB[i]])
                eng = "dve" if c % 2 == 0 else "pool"
                K.op(eng, lambda e, c=c, i=i: e.tensor_tensor(xc(X, c), xc(X, c), TMP[i][:, :], ALU.add), [TMPB[i], XB[c]], [XB[c]])
                if next_sq:
                    sq_from(xc(X, c), XB[c], c)
                    sumsq_mm(c)

        def proj_group(bk, bkb, unit, unitb, col0=0, ncols=T):
            for kc in range(8):
                mm(bk[:, col0:col0 + ncols], unit[:, kc * 128:(kc + 1) * 128], xc(H, kc), start=(kc == 0), stop=(kc == 7),
                   reads=[unitb, HB[kc]], writes=[bkb])
            next_unit()

        def proj_multi(specs):
            for kc in range(8):
                for (bk, bkb, unit, unitb) in specs:
                    mm(bk[:, :], unit[:, kc * 128:(kc + 1) * 128], xc(H, kc), start=(kc == 0), stop=(kc == 7),
                       reads=[unitb, HB[kc]], writes=[bkb])
            for _ in specs:
                next_unit()

        def make_tables(t):
            i = t % 2
            dma("sp", POSI[:, :], pos_d[:, t * T:(t + 1) * T], [], [POSIB], d_pos)
            POSF, TT_, KF, FR = TMP[0], TMP[1], RS, RT
            TABB = [TMPB[0], TMPB[1], RSB, RTB]
            K.op("dve", lambda e: e.tensor_copy(POSF[:, :], POSI[:, :]), [POSIB], TABB)
            for which in range(2):
                addc = 0.0 if which == 0 else 0.25
                K.op("dve", lambda e, addc=addc: e.tensor_scalar(TT_[:, :], POSF[:, :], CST[:, 0:1], addc, ALU.mult, ALU.add), TABB + [CONSTB], TABB)
                K.op("dve", lambda e: e.tensor_scalar(KF[:, :], TT_[:, :], MAGIC, MAGIC, ALU.add, ALU.subtract), TABB, TABB)
                K.op("dve", lambda e: e.tensor_tensor(FR[:, :], TT_[:, :], KF[:, :], ALU.subtract), TABB, TABB)
                if which == 0:
                    K.op("act", lambda e: e.activation(STAB[i][:, :], FR[:, :], AF.Sin, scale=CST[:, 1:2]), TABB + [CONSTB], [CSB[i]] + TABB)
                else:
                    K.op("act", lambda e: e.activation(CTAB[i][:, :], FR[:, :], AF.Sin, scale=TWO_PI_SAFE), TABB, [CSB[i]] + TABB)

        def tile_layer(t, l):
            first = (t == 0)
            X, XB = XT[t % 2], XTB[t % 2]
            ti = t % 2
            v0, d0, m0 = l * VL, l * 32, l * 48

            prenorm(X, XB, d0 + 0, m0 + 0, squares_done=(l > 0))
            qspecs = []
            for g in range(4):
                bk, bkb = bank()
                unit, unitb = W(t, l, g)
                qspecs.append((bk, bkb, unit, unitb))
            proj_multi(qspecs)
            kb, kbb = bank()
            unit, unitb = W(t, l, U_K)
            proj_group(kb, kbb, unit, unitb)
            qk_banks = [(s[0], s[1]) for s in qspecs] + [(kb, kbb)]
            for g in range(5):
                bk, bkb = qk_banks[g]
                bcol = v0 + 32 + g
                K.op("act", lambda e, bk=bk, g=g, bcol=bcol: e.activation(QB[g][:, :], bk[:, :], AF.Identity, bias=VEC[:, bcol:bcol + 1], scale=1.0),
                     [bkb, CONSTB], [QBB[g]])
            bk, bkb = bank()
            unit, unitb = W(t, l, U_V)
            for blk in range(4):
                for kc in range(8):
                    mm(bk[:, blk * 128:(blk + 1) * 128], H[:, kc * T + blk * 128: kc * T + (blk + 1) * 128], unit[:, kc * 128:(kc + 1) * 128],
                       start=(kc == 0), stop=(kc == 7), reads=[unitb, HB[kc]], writes=[bkb], signal=(kc == 7 and blk == 3))
            next_unit()
            K.op("dve", lambda e, bk=bk: e.tensor_tensor(VT[l][:, 128:640], bk[:, :], BVR[:, l * 512:(l + 1) * 512], ALU.add), [bkb, CONSTB], [VTB[l]])
            for g in range(5):
                i = g % 2
                b2, b2b = bank()
                mm(b2[:, :], PERM, QB[g][:, :], start=True, stop=True, reads=[QBB[g], CONSTB], writes=[b2b])
                K.op("dve", lambda e, b2=b2, i=i: e.tensor_tensor(TR[i][:, :], b2[:, :], STAB[ti][:, :], ALU.mult), [b2b, CSB[ti]], [TRB[i]])
                K.op("pool", lambda e, i=i, g=g: e.tensor_tensor(QC[i][:, :], QB[g][:, :], CTAB[ti][:, :], ALU.mult), [QBB[g], CSB[ti]], [QCB[i]])
                if g < 4:
                    K.op("pool", lambda e, i=i, g=g: e.tensor_tensor(xc(QT, g), QC[i][:, :], TR[i][:, :], ALU.add), [QCB[i], TRB[i]], [QTB[g]])
                else:
                    K.op("pool", lambda e, i=i: e.tensor_tensor(KT[l][0:64, 128:640], QC[i][0:64, :], TR[i][0:64, :], ALU.add),
                         [QCB[i], TRB[i]], [KTB[l]])
                    K.op("pool", lambda e, i=i: e.tensor_tensor(KT[l][64:128, 640 + 128:640 + 640], QC[i][64:128, :], TR[i][64:128, :], ALU.add),
                         [QCB[i], TRB[i]], [KTB[l]])
            for gi in range(4):
                bk, bkb = bank()
                unit, unitb = W(t, l, U_U0 + gi)
                proj_group(bk, bkb, unit, unitb)
                bcol = v0 + 37 + gi
                K.op("act", lambda e, bk=bk, gi=gi, bcol=bcol: e.activation(UT[:, gi * 528 + 16: gi * 528 + 528], bk[:, :], AF.Identity,
                                                                               bias=VEC[:, bcol:bcol + 1], scale=1.0), [bkb, CONSTB], [UTB[gi]])
            UT3 = UT[:, :].rearrange("p (g w) -> p g w", g=4)
            K.op("pool", lambda e: e.tensor_copy(UT3[:, :, 0:16], UH[l][:, :].rearrange("p (g w) -> p g w", g=4)), [UHB[l]], UTB)
            for gi in range(4):
                w = 2 << gi
                cur, curb = UT[:, gi * 528:(gi + 1) * 528], UTB[gi]
                for k in range(1, gi + 2):
                    sh = 1 << (k - 1)
                    dst, dstb = LV[k % 2], LVB[k % 2]
                    K.op("pool", lambda e, cur=cur, dst=dst, sh=sh: e.tensor_tensor(dst[:, sh:528], cur[:, sh:528], cur[:, 0:528 - sh], ALU.add),
                         [curb], [dstb])
                    cur, curb = dst[:, :], dstb
                K.op("dve", lambda e, cur=cur, gi=gi, w=w: e.scalar_tensor_tensor(
                    xc(PL, gi), cur[:, 16:528], 1.0 / w, UT[:, gi * 528 + 16:gi * 528 + 528], ALU.mult, ALU.subtract), [curb, UTB[gi]], [PLB[gi]])
                if first:
                    K.op("pool", lambda e, cur=cur, gi=gi: e.tensor_tensor(T16[:, :], cur[:, 16:32], INVC[:, gi * 16:(gi + 1) * 16], ALU.mult),
                         [curb, CONSTB], [T16B])
                    K.op("pool", lambda e, gi=gi: e.tensor_tensor(PL[:, gi * T:gi * T + 16], T16[:, :], UT[:, gi * 528 + 16:gi * 528 + 32], ALU.subtract),
                         [T16B, UTB[gi]], [PLB[gi]])
            K.op("pool", lambda e: e.tensor_copy(UH[l][:, :].rearrange("p (g w) -> p g w", g=4), UT3[:, :, 512:528]), UTB, [UHB[l]])

            pr_i = [0]

            def scores(j):
                for kv in range(2):
                    for pc in range(2):
                        if first and j == 0 and pc == 0:
                            continue
                        bk, bkb = bank()
                        kcol = kv * 640 + (j + pc) * 128
                        for g in range(4):
                            mm(bk[:, g * 128:(g + 1) * 128], KT[l][:, kcol:kcol + 128], QT[:, g * T + j * 128: g * T + (j + 1) * 128],
                               start=True, stop=True, reads=[KTB[l], QTB[g]], writes=[bkb], signal=(g == 3))
                        r = pr_i[0] % 2
                        pr_i[0] += 1
                        K.op("act", lambda e, bk=bk, r=r: e.activation(PR[r][:, :], bk[:, :], AF.Exp, scale=0.125), [bkb], [PRB[r]])
                        pm = (j % 2) * 4 + kv * 2 + pc
                        moff = ((l * 2 + kv) * 2 + pc) * T
                        K.op("dve", lambda e, r=r, pm=pm, moff=moff: e.tensor_tensor(PM[pm][:, :], PR[r][:, :], MK[:, moff:moff + T], ALU.mult),
                             [PRB[r]] + BARRIER, [PMB[pm]])

            def pv(j):
                pcs = [1] if (first and j == 0) else [0, 1]
                ob, obb = bank()
                db, dbb = bank()
                for pc in pcs:
                    for kv in range(2):
                        pm = (j % 2) * 4 + kv * 2 + pc
                        last = (pc == pcs[-1] and kv == 1)
                        mm(db[kv * 64:(kv + 1) * 64, :], ONES1, PM[pm][:, :], start=(pc == pcs[0]), stop=(pc == pcs[-1]),
                           reads=[CONSTB, PMB[pm]], writes=[dbb], signal=last, tp=(0, kv * 64))
                for pc in pcs:
                    for kv in range(2):
                        pm = (j % 2) * 4 + kv * 2 + pc
                        vcol = (j + pc) * 128 + kv * 64
                        last = (pc == pcs[-1] and kv == 1)
                        mm(ob[kv * 64:(kv + 1) * 64, :], VT[l][:, vcol:vcol + 64], PM[pm][:, :], start=(pc == pcs[0]), stop=(pc == pcs[-1]),
                           reads=[VTB[l], PMB[pm]], writes=[obb], signal=last, tp=(0, kv * 64))
                K.op("act", lambda e, db=db: e.activation(RT[:, :], db[:, :], AF.Ln, bias=CST[:, 4:5], scale=1.0), [dbb, CONSTB], [RTB])
                K.op("act", lambda e: e.activation(RT[:, :], RT[:, :], AF.Exp, scale=-1.0), [RTB], [RTB])
                AT3 = AT[:, :].rearrange("p (g t) -> p g t", g=4)
                K.op("dve", lambda e, ob=ob, j=j: e.tensor_tensor(AT3[:, :, j * 128:(j + 1) * 128],
                                                                   ob[:, :].rearrange("p (g q) -> p g q", g=4),
                                                                   RT[:, :].rearrange("p (g q) -> p g q", g=4), ALU.mult), [obb, RTB], ATB)

            scores(0)
            for j in range(4):
                if j + 1 < 4:
                    scores(j + 1)
                pv(j)
            punit, punitb = W(t, l, U_POOL)
            for gi in range(4):
                bk, bkb = bank()
                mm(bk[:, :], punit[:, gi * 128:(gi + 1) * 128], xc(PL, gi), start=True, stop=True, reads=[punitb, PLB[gi]], writes=[bkb])
                if gi == 3:
                    next_unit()
                scol = v0 + 41 + gi
                K.op("act", lambda e, bk=bk, gi=gi, scol=scol: e.activation(xc(PO, gi), bk[:, :], AF.Identity, scale=VEC[:, scol:scol + 1]),
                     [bkb, CONSTB], [POB[gi]])
            KT3 = KT[l][:, :].rearrange("p (m w) -> p m w", m=2)
            K.op("pool", lambda e: e.tensor_copy(KT3[:, :, 0:128], KT3[:, :, 512:640]), [KTB[l]], [KTB[l]])
            K.op("pool", lambda e: e.tensor_copy(VT[l][:, 0:128], VT[l][:, 512:640]), [VTB[l]], [VTB[l]])

            for o in range(8):
                bk, bkb = bank()
                unit, unitb = W(t, l, U_WO0 + o)
                for kc in range(8):
                    rhs, rb = (xc(AT, kc), ATB[kc]) if kc < 4 else (xc(PO, kc - 4), POB[kc - 4])
                    mm(bk[:, :], unit[:, kc * 128:(kc + 1) * 128], rhs, start=(kc == 0), stop=(kc == 7), reads=[unitb, rb], writes=[bkb])
                next_unit()
                K.op("act", lambda e, bk=bk, o=o: e.activation(xc(MIXF, o), bk[:, :], AF.Identity), [bkb], [MIXB[o]])
                sq_from(bk[:, :], bkb, o)
                if o >= 1:
                    sumsq_mm(o - 1)
            sumsq_mm(7)
            postnorm_residual(X, XB, d0 + 8, next_sq=True)

            prenorm(X, XB, d0 + 16, m0 + 24, squares_done=True)

            def ffn_evac(f, bg, bgb, bu, bub):
                i = f % 2
                K.op("act", lambda e: e.activation(SG[i][:, :], bg[:, :], AF.Silu), [bgb], [SGB[i]])
                K.op("dve", lambda e: e.tensor_tensor(xc(ACTB, f), bu[:, :], SG[i][:, :], ALU.mult), [bub, SGB[i]], [ACTBB[f]])

            NF0 = 3
            specs = []
            for f in range(NF0):
                for k2 in range(2):
                    bk, bkb = bank()
                    unit, unitb = W(t, l, U_G0 + 2 * f + k2)
                    specs.append((bk, bkb, unit, unitb))
            proj_multi(specs)
            for f in range(NF0):
                ffn_evac(f, specs[2 * f][0], specs[2 * f][1], specs[2 * f + 1][0], specs[2 * f + 1][1])
            for f in range(NF0, NF):
                bg, bgb = bank()
                unit, unitb = W(t, l, U_G0 + 2 * f)
                proj_group(bg, bgb, unit, unitb)
                bu, bub = bank()
                unit, unitb = W(t, l, U_G0 + 2 * f + 1)
                proj_group(bu, bub, unit, unitb)
                ffn_evac(f, bg, bgb, bu, bub)
            for o in range(8):
                bk, bkb = bank()
                s = wslot_of[(t, l, o)]
                for f in range(NF):
                    mm(bk[:, :], WDR[s][:, f * 128:(f + 1) * 128], xc(ACTB, f), start=(f == 0), stop=(f == NF - 1),
                       reads=[WDRB[s], ACTBB[f]], writes=[bkb])
                next_wd()
                K.op("act", lambda e, bk=bk, o=o: e.activation(xc(MIXF, o), bk[:, :], AF.Identity), [bkb], [MIXB[o]])
                sq_from(bk[:, :], bkb, o)
                if o >= 1:
                    sumsq_mm(o - 1)
            sumsq_mm(7)
            postnorm_residual(X, XB, d0 + 24, next_sq=(l + 1 < L))

        dma("sp", XT[0][:, :], xT_d[0, :, :], [], XTB[0], d_x[0])
        for _ in range(NSLOT):
            next_unit()
        for _ in range(NWD):
            next_wd()
        for t in range(NT):
            if t + 1 < NT:
                i = (t + 1) % 2
                dma("sp", XT[i][:, :], xT_d[t + 1, :, :], [], XTB[i], d_x[i])
            make_tables(t)
            for l in range(L):
                tile_layer(t, l)
            dma("pool", out_d[t, :, :], XT[t % 2][:, :], XTB[t % 2], [OUTB], d_out)
        K.streams["pool"].ops.append(lambda e: e.wait_ge(d_out.h, d_out.count))
        if debug:
            K.streams["act"].ops.append(lambda e: e.wait_ge(d_dbg.h, d_dbg.count))

        with nc.Block() as block:
            @block.sync
            def _(e):
                for f in K.streams["sp"].ops:
                    f(e)

            @block.tensor
            def _(e):
                for f in K.streams["pe"].ops:
                    f(e)

            @block.scalar
            def _(e):
                for f in K.streams["act"].ops:
                    f(e)

            @block.vector
            def _(e):
                for f in K.streams["dve"].ops:
                    f(e)

            @block.gpsimd
            def _(e):
                for f in K.streams["pool"].ops:
                    f(e)
    return nc


def _unit(Wm, rows, cols):
    sub = Wm[np.asarray(rows)][:, np.asarray(cols)]
    return sub.reshape(8, 128, 128).transpose(1, 0, 2)


def pack_shared(inp, L=2):
    f32 = np.float32
    ada_w, ada_b = np.asarray(inp["ada_w"], f32), np.asarray(inp["ada_b"], f32)
    w_in, b_in = np.asarray(inp["w_in"], f32), np.asarray(inp["b_in"], f32)
    sinks = np.asarray(inp["sinks"], f32)
    pool_w, pool_scale = np.asarray(inp["pool_w"], f32), np.asarray(inp["pool_scale"], f32)
    w_out = np.asarray(inp["w_out"], f32)
    w_gate, w_up, w_down = np.asarray(inp["w_gate"], f32), np.asarray(inp["w_up"], f32), np.asarray(inp["w_down"], f32)
    gs = [np.asarray(inp[k], f32) for k in ("g_pre_mix", "g_post_mix", "g_pre_ffn", "g_post_ffn")]

    adaw = np.empty((L * 24, 128, 2, 8, 128), f32)
    for l in range(L):
        adaw[l * 24:(l + 1) * 24] = ada_w[l].reshape(8, 128, 24, 2, 128).transpose(2, 1, 3, 0, 4)
    adaw = adaw.reshape(L * 24, 128, 2048)
    adab = np.concatenate([ada_b[l].reshape(48, 128).T for l in range(L)], axis=1)

    nat = np.arange(1024)
    qcols = [np.concatenate([np.arange(g * 64, g * 64 + 64), np.arange((g + 4) * 64, (g + 4) * 64 + 64)]) for g in range(4)]
    mixrows = np.concatenate([qcols[g] for g in range(4)] + [np.arange(512, 1024)])
    wu = np.zeros((L, NU, 128, 8, 128), f32)
    wd = np.empty((L, 8, 128, NF, 128), f32)
    for l in range(L):
        for g in range(4):
            wu[l, g] = _unit(w_in[l], nat, qcols[g])
        wu[l, U_K] = _unit(w_in[l], nat, np.arange(512, 640))
        wu[l, U_V] = _unit(w_in[l], nat, np.arange(640, 768))
        for gi in range(4):
            wu[l, U_U0 + gi] = _unit(w_in[l], nat, np.arange(768 + gi * 128, 768 + (gi + 1) * 128))
            wu[l, U_POOL, :, gi, :] = pool_w[l, gi]
        for o in range(8):
            wu[l, U_WO0 + o] = _unit(w_out[l], mixrows, np.arange(o * 128, (o + 1) * 128))
            wd[l, o] = w_down[l][:, o * 128:(o + 1) * 128].reshape(NF, 128, 128).transpose(1, 0, 2)
        for f in range(NF):
            wu[l, U_G0 + 2 * f] = _unit(w_gate[l], nat, np.arange(f * 128, (f + 1) * 128))
            wu[l, U_G0 + 2 * f + 1] = _unit(w_up[l], nat, np.arange(f * 128, (f + 1) * 128))
    wu = wu.reshape(-1, 16384)
    wd = wd.reshape(-1, 11264)

    def pcol(v):
        return v.reshape(-1, 128).T

    vecs = np.empty((128, L * VL), f32)
    bvrep = np.empty((128, L * 512), f32)
    sinkrep = np.empty((128, L * 8), f32)
    for l in range(L):
        v0 = l * VL
        for i in range(4):
            vecs[:, v0 + 8 * i:v0 + 8 * i + 8] = pcol(gs[i][l])
        for g in range(4):
            vecs[:, v0 + 32 + g] = b_in[l][qcols[g]]
        vecs[:, v0 + 36] = b_in[l][512:640]
        vecs[:, v0 + 37:v0 + 41] = pcol(b_in[l][768:1280])
        vecs[:, v0 + 41:v0 + 45] = pcol(pool_scale[l])
        bvrep[:, l * 512:(l + 1) * 512] = np.tile(b_in[l][640:768], (128, 4))
        sinkrep[:, l * 8:(l + 1) * 8] = np.tile(sinks[l], (128, 1))

    cst = np.zeros((128, 8), f32)
    inv_freq = (ROPE_THETA ** (-np.arange(0, 16, 2, dtype=np.float32) / 16)).astype(f32)
    for p in range(128):
        r = p % 64
        if r < 16:
            cst[p, 0] = np.float32(inv_freq[r % 8]) / np.float32(2 * np.pi)
            cst[p, 1] = (-1.0 if r < 8 else 1.0) * TWO_PI_SAFE
    cst[:, 2] = TWO_PI_SAFE
    cst[:, 3] = EPS
    cst[:, 4] = 1.0
    invc = np.zeros((128, 64), f32)
    for gi in range(4):
        w = 2 << gi
        invc[:, gi * 16:(gi + 1) * 16] = 1.0 / np.minimum(np.arange(16) + 1.0, float(w))
    cmat = np.zeros((128, 576), f32)
    cmat[:, 0:128] = 1.0 / 1024.0
    cmat[:, 128:192] = 1.0
    for m in range(128):
        r = m % 64
        if r < 8:
            cmat[m + 8, 192 + m] = 1.0
        elif r < 16:
            cmat[m - 8, 192 + m] = 1.0
    s_idx = np.arange(128)[:, None]
    q_idx = np.arange(128)[None, :]
    cmat[:, 320:448] = (s_idx > q_idx).astype(f32)
    cmat[:, 448:576] = (s_idx <= q_idx).astype(f32)
    return dict(adaw=adaw, adab=np.ascontiguousarray(adab), wu=wu, wd=wd, vecs=vecs, bvrep=bvrep, sinkrep=sinkrep,
                cst=cst, invc=invc, cmat=cmat)


def pack_core(x_b, c_b, pos_b):
    S = x_b.shape[0]
    NT = S // T
    xT = np.ascontiguousarray(np.asarray(x_b, np.float32).reshape(NT, T, 8, 128).transpose(0, 3, 2, 1)).reshape(NT, 128, 8 * T)
    pos = np.ascontiguousarray(np.broadcast_to(np.asarray(pos_b, np.int32)[None, :], (128, S)))
    cT = np.ascontiguousarray(np.asarray(c_b, np.float32).reshape(8, 128).T)
    return dict(xT=xT, pos=pos, cT=cT)


def unpack_out(outT, S):
    NT = S // T
    return outT.reshape(NT, 128, 8, T).transpose(0, 3, 2, 1).reshape(S, D)


_CACHE = {}


def kernel(_debug=False, **inputs):
    x = np.asarray(inputs["x"])
    B, S, _ = x.shape
    shared = pack_shared(inputs)
    in_maps = []
    for b in range(B):
        m = dict(shared)
        m.update(pack_core(x[b], np.asarray(inputs["c"])[b], np.asarray(inputs["positions"])[b]))
        in_maps.append(m)
    key = (S, _debug)
    if key not in _CACHE:
        _CACHE[key] = build_program(S, debug=_debug)
    nc = _CACHE[key]
    res = run_bass_kernel_spmd(nc, in_maps, core_ids=list(range(B)))
    if _debug:
        kernel.dbg = [np.where(np.arange(24)[:, None, None] >= 0, 0, 0) for b in range(0)]
        for b in range(B):
            d32 = np.asarray(res.results[b]["dbg"]).astype(np.float32)
            d16 = np.asarray(res.results[b]["dbgh"]).astype(np.float32)
            halfidx = [2, 3, 4, 5, 6, 8, 9, 15, 16, 10, 13]
            for i in halfidx:
                d32[i] = d16[i]
            kernel.dbg.append(d32)
    out = np.empty((B, S, D), np.float32)
    for b in range(B):
        out[b] = unpack_out(np.asarray(res.results[b]["outT"]), S)
    return out
```

```python
import numpy as np
from contextlib import ExitStack
import concourse.bass as bass
import concourse.mybir as mybir
from concourse.bass_utils import run_bass_kernel_spmd

F32 = mybir.dt.float32
BF16 = mybir.dt.bfloat16
I32 = mybir.dt.int32
ALU = mybir.AluOpType
AF = mybir.ActivationFunctionType

D = 1024
T = 512
DFF = 2816
NF = 22
NU = 63
VL = 45
NSLOT = 8
NWD = 3
EPS = 1e-6
MAGIC = 12582912.0
TWO_PI_SAFE = 6.28318
ROPE_THETA = 500000.0

U_K, U_V, U_U0, U_POOL, U_WO0, U_G0 = 4, 5, 6, 10, 11, 19


class Buf:
    __slots__ = ("w", "r", "const", "name")

    def __init__(self, name="", const=False):
        self.w = None
        self.r = {}
        self.const = const
        self.name = name


class DmaSem:
    def __init__(self, h):
        self.h = h
        self.count = 0


class Stream:
    def __init__(self, name, sem):
        self.name = name
        self.sem = sem
        self.count = 0
        self.ops = []
        self.seen = {}


class Tracker:
    def __init__(self):
        self.streams = {}

    def op(self, en, fn, reads=(), writes=(), signal=True, dma=None):
        st = self.streams[en]
        deps = []
        for b in reads:
            if b.w is not None:
                deps.append(b.w)
        for b in writes:
            if b.w is not None:
                deps.append(b.w)
            deps.extend(b.r.values())
        waits = {}
        for (sem, val) in deps:
            if en == "pe" and sem is st.sem:
                continue
            key = id(sem)
            if st.seen.get(key, 0) >= val:
                continue
            if key not in waits or waits[key][1] < val:
                waits[key] = (sem, val)
        for sem, val in waits.values():
            st.seen[id(sem)] = val
            st.ops.append(lambda e, sem=sem, val=val: e.wait_ge(sem, val))
        if dma is not None:
            dma.count += 16
            ev = (dma.h, dma.count)
            st.ops.append(lambda e, fn=fn, s=dma.h: fn(e).then_inc(s, 16))
        elif signal:
            st.count += 1
            ev = (st.sem, st.count)
            st.ops.append(lambda e, fn=fn, s=st.sem: fn(e).then_inc(s, 1))
        else:
            ev = (st.sem, st.count + 1)
            st.ops.append(lambda e, fn=fn: fn(e))
        for b in writes:
            b.w = ev
            b.r = {}
        for b in reads:
            if not b.const:
                k = id(ev[0])
                if k not in b.r or b.r[k][1] < ev[1]:
                    b.r[k] = ev
        return ev


def build_program(S, L=2, debug=False):
    NT = S // T
    nc = bass.Bass("TRN2", target_bir_lowering=False)

    def din(name, shape, dt=F32):
        return nc.dram_tensor(name, shape, dt, kind="ExternalInput").ap()

    xT_d = din("xT", [NT, 128, 8 * T])
    pos_d = din("pos", [128, S], I32)
    cT_d = din("cT", [128, 8])
    adaw_d = din("adaw", [L * 24, 128, 2048])
    adab_d = din("adab", [128, L * 48])
    wu_d = din("wu", [L * NU * 128 * 1024 // 16384, 16384])
    wd_d = din("wd", [L * 8 * 32, 11264])
    vec_d = din("vecs", [128, L * VL])
    bv_d = din("bvrep", [128, L * 512])
    snk_d = din("sinkrep", [128, L * 8])
    cst_d = din("cst", [128, 8])
    invc_d = din("invc", [128, 64])
    cmat_d = din("cmat", [128, 576])
    out_d = nc.dram_tensor("outT", [NT, 128, 8 * T], F32, kind="ExternalOutput").ap()
    dbg_d = nc.dram_tensor("dbg", [24, 128, 4096], F32, kind="ExternalOutput").ap() if debug else None
    dbgh_d = nc.dram_tensor("dbgh", [24, 128, 4096], BF16, kind="ExternalOutput").ap() if debug else None
    wub_d = nc.dram_tensor("wub", [L * NU * 128 * 1024 // 16384, 16384], BF16, kind="Internal").ap()
    wdb_d = nc.dram_tensor("wdb", [L * 8 * 32, 11264], BF16, kind="Internal").ap()

    K = Tracker()
    with ExitStack() as es:
        def sb(name, cols, dt=F32):
            return es.enter_context(nc.sbuf_tensor(name, [128, cols], dt))

        def sem(name):
            return es.enter_context(nc.semaphore(name))

        XT = [sb(f"xt{i}", 8 * T) for i in range(2)]
        XTB = [[Buf(f"x{i}_{c}") for c in range(8)] for i in range(2)]
        H = sb("h", 8 * T, BF16)
        HB = [Buf(f"h{c}") for c in range(8)]
        SQ = [sb(f"sq{i}", T, BF16) for i in range(4)]
        SQB = [Buf() for _ in range(4)]
        RS = sb("rs", T)
        RSB = Buf("rs")
        TMP = [sb(f"tmp{i}", T) for i in range(4)]
        TMPB = [Buf() for _ in range(4)]
        QB = [sb(f"qb{i}", T, BF16) for i in range(5)]
        QBB = [Buf() for _ in range(5)]
        QC = [sb(f"qc{i}", T, BF16) for i in range(2)]
        QCB = [Buf() for _ in range(2)]
        TR = [sb(f"tr{i}", T, BF16) for i in range(2)]
        TRB = [Buf() for _ in range(2)]
        QT = sb("qt", 4 * T, BF16)
        QTB = [Buf(f"qt{g}") for g in range(4)]
        KT = [sb(f"kt{l}", 2 * 640, BF16) for l in range(L)]
        KTB = [Buf(f"kt{l}") for l in range(L)]
        VT = [sb(f"vt{l}", 640, BF16) for l in range(L)]
        VTB = [Buf(f"vt{l}") for l in range(L)]
        UT = sb("ut", 4 * 528)
        UTB = [Buf(f"ut{g}") for g in range(4)]
        UH = [sb(f"uh{l}", 64) for l in range(L)]
        UHB = [Buf(f"uh{l}") for l in range(L)]
        LV = [sb(f"lv{i}", 528) for i in range(2)]
        LVB = [Buf() for _ in range(2)]
        T16 = sb("t16", 16)
        T16B = Buf()
        PL = sb("pl", 4 * T, BF16)
        PLB = [Buf() for _ in range(4)]
        PO = sb("po", 4 * T, BF16)
        POB = [Buf() for _ in range(4)]
        AT = sb("at", 4 * T, BF16)
        ATB = [Buf() for _ in range(4)]
        PR = [sb(f"pr{i}", T, BF16) for i in range(2)]
        PRB = [Buf() for _ in range(2)]
        PM = [sb(f"pm{i}", T, BF16) for i in range(8)]
        PMB = [Buf() for _ in range(8)]
        RT = sb("rt", T)
        RTB = Buf()
        MIXF = sb("mixf", 8 * T)
        MIXB = [Buf(f"mix{c}") for c in range(8)]
        ACTB = sb("actb", NF * T, BF16)
        ACTBB = [Buf() for _ in range(NF)]
        SG = [sb(f"sg{i}", T, BF16) for i in range(2)]
        SGB = [Buf() for _ in range(2)]
        CTAB = [sb(f"ctab{i}", T, BF16) for i in range(2)]
        STAB = [sb(f"stab{i}", T, BF16) for i in range(2)]
        CSB = [Buf() for _ in range(2)]
        POSI = sb("posi", T, I32)
        POSIB = Buf()
        MK = sb("mk", L * 4 * T, BF16)
        MKB = Buf(const=False)
        RING = [sb(f"ring{i}", 1024, BF16) for i in range(NSLOT)]
        RINGB = [Buf(f"ring{i}") for i in range(NSLOT)]
        WDR = [sb(f"wdr{i}", DFF, BF16) for i in range(NWD)]
        WDRB = [Buf(f"wdr{i}") for i in range(NWD)]
        CT = sb("ct", 8)
        CA = sb("ca", 8)
        CAB = Buf()
        ADAB = sb("adab_s", L * 48)
        MOD = sb("mod", L * 48)
        MODB = Buf()
        DER = sb("der", L * 32)
        DERB = Buf()
        VEC = sb("vec", L * VL)
        BVR = sb("bvr", L * 512)
        SNK = sb("snk", L * 8)
        ES = sb("es", L * 8)
        CST = sb("cst_s", 8)
        INVC = sb("invc_s", 64)
        CMATF = MIXF[:, 0:576]
        CMAT = sb("cmat_s", 576, BF16)
        CONSTB = Buf("const")
        ONESM = CMAT[:, 0:128]
        ONES1 = CMAT[:, 128:192]
        PERM = CMAT[:, 192:320]

        PS = [es.enter_context(nc.psum_tensor(f"ps{i}", [128, 512], F32)) for i in range(8)]
        PSB = [Buf(f"ps{i}") for i in range(8)]

        s_pe, s_act, s_dve, s_pool, s_sp = sem("s_pe"), sem("s_act"), sem("s_dve"), sem("s_pool"), sem("s_sp")
        K.streams = {
            "pe": Stream("pe", s_pe), "act": Stream("act", s_act), "dve": Stream("dve", s_dve),
            "pool": Stream("pool", s_pool), "sp": Stream("sp", s_sp),
        }
        d_x = [DmaSem(sem(f"d_x{i}")) for i in range(2)]
        d_pos = DmaSem(sem("d_pos"))
        d_ring = [DmaSem(sem(f"d_ring{i}")) for i in range(NSLOT)]
        d_wd = [DmaSem(sem(f"d_wd{i}")) for i in range(NWD)]
        d_const = DmaSem(sem("d_const"))
        d_aw = [DmaSem(sem(f"d_aw{i}")) for i in range(2)]
        d_cast = DmaSem(sem("d_cast"))
        d_out = DmaSem(sem("d_out"))
        CASTB = Buf("cast")
        OUTB = Buf("outdram")

        bank_rr = [0]

        def bank():
            i = bank_rr[0] % 6
            bank_rr[0] += 1
            return PS[i], PSB[i]

        NRM, NRMB = PS[7], PSB[7]
        NRM2, NRM2B = PS[6], PSB[6]

        def mm(out, lhsT, rhs, start, stop, reads, writes, signal=None, tp=None):
            if signal is None:
                signal = stop
            if tp is None:
                fn = lambda e: e.matmul(out, lhsT, rhs, start=start, stop=stop)
            else:
                fn = lambda e: e.matmul(out, lhsT, rhs, start=start, stop=stop, tile_position=tp)
            K.op("pe", fn, reads, writes, signal)

        def dma(en, out, in_, reads, writes, dsem):
            K.op(en, lambda e: e.dma_start(out=out, in_=in_), reads, writes, dma=dsem)

        d_dbg = DmaSem(sem("d_dbg"))
        DBGB = Buf("dbg")

        def dump(idx, ap, cols, bufs, half=False):
            if debug:
                dst = dbgh_d if half else dbg_d
                dma("act", dst[idx, :, 0:cols], ap, list(bufs), [DBGB], d_dbg)

        nrow_u = L * NU * 128 * 1024 // 16384
        nrow_d = L * 8 * 32
        step = 63
        for r0 in range(0, nrow_u, step):
            r1 = min(nrow_u, r0 + step)
            dma("pool", wub_d[r0:r1, :], wu_d[r0:r1, :], [], [], d_cast)
        step = 64
        for r0 in range(0, nrow_d, step):
            r1 = min(nrow_d, r0 + step)
            dma("pool", wdb_d[r0:r1, :], wd_d[r0:r1, :], [], [], d_cast)
        CASTB.w = (d_cast.h, d_cast.count)

        for (dst, src) in ((CT, cT_d), (ADAB, adab_d), (VEC, vec_d), (BVR, bv_d), (SNK, snk_d), (CST, cst_d),
                           (INVC, invc_d)):
            dma("sp", dst[:, :], src[:, :], [], [CONSTB], d_const)
        dma("sp", MIXF[:, 0:576], cmat_d[:, :], [], [CONSTB, MIXB[0], MIXB[1]], d_const)
        K.op("dve", lambda e: e.tensor_copy(CMAT[:, :], MIXF[:, 0:576]), [CONSTB, MIXB[0], MIXB[1]], [CONSTB])
        CONSTB.const = False
        K.op("pool", lambda e: e.memset(UT[:, :], 0.0), [], UTB)
        for l in range(L):
            K.op("pool", lambda e, l=l: e.memset(KT[l][:, :], 0.0), [], [KTB[l]])
            K.op("pool", lambda e, l=l: e.memset(VT[l][:, :], 0.0), [], [VTB[l]])
            K.op("pool", lambda e, l=l: e.memset(UH[l][:, :], 0.0), [], [UHB[l]])
        for i in range(2):
            K.op("pool", lambda e, i=i: e.memset(LV[i][:, :], 0.0), [], [LVB[i]])
        K.op("act", lambda e: e.activation(CA[:, :], CT[:, :], AF.Silu), [CONSTB], [CAB])
        K.op("act", lambda e: e.activation(ES[:, :], SNK[:, :], AF.Exp, scale=-1.0), [CONSTB], [CONSTB])
        for l in range(L):
            for kv in range(2):
                for pc in range(2):
                    off = ((l * 2 + kv) * 2 + pc) * T
                    for g in range(4):
                        col = l * 8 + kv * 4 + g
                        K.op("dve", lambda e, off=off, g=g, pc=pc, col=col: e.tensor_scalar(
                            MK[:, off + g * 128: off + (g + 1) * 128], MIXF[:, 320 + pc * 128: 320 + (pc + 1) * 128],
                            ES[:, col:col + 1], 0.0, ALU.mult, ALU.add), [CONSTB, MIXB[0], MIXB[1]], [MKB])
        modbank, modbankb = bank()
        for l in range(L):
            for pc in range(24):
                i = pc % 2
                stg = MIXF[:, i * 2048:(i + 1) * 2048]
                stgb = MIXB[i * 4:(i + 1) * 4]
                dma("sp", stg, adaw_d[l * 24 + pc, :, :], [], stgb, d_aw[i])
                for o in range(2):
                    col = l * 48 + pc * 2 + o
                    for kc in range(8):
                        mm(modbank[:, col:col + 1], stg[:, (o * 8 + kc) * 128:(o * 8 + kc + 1) * 128], CA[:, kc:kc + 1],
                           start=(kc == 0), stop=(kc == 7), reads=stgb + [CAB], writes=[modbankb])
        K.op("dve", lambda e: e.tensor_tensor(MOD[:, :], modbank[:, 0:L * 48], ADAB[:, :], ALU.add),
             [modbankb, CONSTB], [MODB])
        for l in range(L):
            m0 = l * 48
            v0 = l * VL
            d0 = l * 32
            K.op("dve", lambda e, m0=m0, v0=v0, d0=d0: e.scalar_tensor_tensor(
                DER[:, d0:d0 + 8], MOD[:, m0 + 8:m0 + 16], 1.0, VEC[:, v0:v0 + 8], ALU.add, ALU.mult), [MODB, CONSTB], [DERB])
            K.op("dve", lambda e, m0=m0, v0=v0, d0=d0: e.tensor_tensor(
                DER[:, d0 + 8:d0 + 16], MOD[:, m0 + 16:m0 + 24], VEC[:, v0 + 8:v0 + 16], ALU.mult), [MODB, CONSTB], [DERB])
            K.op("dve", lambda e, m0=m0, v0=v0, d0=d0: e.scalar_tensor_tensor(
                DER[:, d0 + 16:d0 + 24], MOD[:, m0 + 32:m0 + 40], 1.0, VEC[:, v0 + 16:v0 + 24], ALU.add, ALU.mult), [MODB, CONSTB], [DERB])
            K.op("dve", lambda e, m0=m0, v0=v0, d0=d0: e.tensor_tensor(
                DER[:, d0 + 24:d0 + 32], MOD[:, m0 + 40:m0 + 48], VEC[:, v0 + 24:v0 + 32], ALU.mult), [MODB, CONSTB], [DERB])
        BARRIER = [MKB, CONSTB, MODB, DERB]
        dump(0, MOD[:, :], L * 48, [MODB])
        dump(1, DER[:, :], L * 32, [DERB])
        dump(2, MK[:, :], L * 4 * T, [MKB], half=True)

        ucount = [0]
        wcount = [0]
        slot_of = {}
        wslot_of = {}

        def load_unit(t, l, u):
            s = ucount[0] % NSLOT
            ucount[0] += 1
            slot_of[(t, l, u)] = s
            r0 = (l * NU + u) * 8
            src = wub_d[r0:r0 + 8, :].rearrange("r (q c) -> (r q) c", q=16)
            dma("sp", RING[s][:, :], src, [CASTB], [RINGB[s]], d_ring[s])

        def load_wd(t, l, o):
            s = wcount[0] % NWD
            wcount[0] += 1
            wslot_of[(t, l, o)] = s
            r0 = (l * 8 + o) * 32
            src = wdb_d[r0:r0 + 32, :].rearrange("r (q c) -> (r q) c", c=DFF)
            dma("sp", WDR[s][:, :], src, [CASTB], [WDRB[s]], d_wd[s])

        ULIST = [(t_, l_, u_) for t_ in range(NT) for l_ in range(L) for u_ in range(NU)]
        WLIST = [(t_, l_, o_) for t_ in range(NT) for l_ in range(L) for o_ in range(8)]
        upos = [0]
        wpos = [0]

        def next_unit():
            if upos[0] < len(ULIST):
                load_unit(*ULIST[upos[0]])
                upos[0] += 1

        def next_wd():
            if wpos[0] < len(WLIST):
                load_wd(*WLIST[wpos[0]])
                wpos[0] += 1

        def W(t, l, u):
            s = slot_of[(t, l, u)]
            return RING[s], RINGB[s]

        def xc(X, c):
            return X[:, c * T:(c + 1) * T]

        def finish_rstd(nrm=None, nrmb=None, rs=None, rsb=None):
            nrm = NRM if nrm is None else nrm
            nrmb = NRMB if nrmb is None else nrmb
            rs = RS if rs is None else rs
            rsb = RSB if rsb is None else rsb
            K.op("act", lambda e: e.activation(rs[:, :], nrm[:, :], AF.Ln, bias=CST[:, 3:4], scale=1.0), [nrmb, CONSTB], [rsb])
            K.op("act", lambda e: e.activation(rs[:, :], rs[:, :], AF.Exp, scale=-0.5), [rsb], [rsb])

        def sq_from(src_ap, src_buf, idx, sqs=None):
            sq, sqb = (SQ, SQB) if sqs is None else sqs
            i = idx % 4
            K.op("act", lambda e: e.activation(sq[i][:, :], src_ap, AF.Square), [src_buf], [sqb[i]])

        def sumsq_mm(idx, nrm=None, nrmb=None, sqs=None):
            nrm = NRM if nrm is None else nrm
            nrmb = NRMB if nrmb is None else nrmb
            sq, sqb = (SQ, SQB) if sqs is None else sqs
            i = idx % 4
            mm(nrm[:, :], ONESM, sq[i][:, :], start=(idx == 0), stop=(idx == 7), reads=[sqb[i], CONSTB], writes=[nrmb], signal=True)

        def prenorm_stats(X, XB, nrm=None, nrmb=None, sqs=None):
            for c in range(8):
                sq_from(xc(X, c), XB[c], c, sqs)
                sumsq_mm(c, nrm, nrmb, sqs)

        def prenorm_h(X, XB, gs_col, sh_col, nrm=None, nrmb=None, rs=None, rsb=None):
            finish_rstd(nrm, nrmb, rs, rsb)
            rs = RS if rs is None else rs
            rsb = RSB if rsb is None else rsb
            for c in range(8):
                i = c % 4
                if c % 2 == 0:
                    K.op("dve", lambda e, c=c, i=i: e.tensor_tensor(TMP[i][:, :], xc(X, c), rs[:, :], ALU.mult), [XB[c], rsb], [TMPB[i]])
                    K.op("act", lambda e, c=c, i=i: e.activation(xc(H, c), TMP[i][:, :], AF.Identity,
                                                                    bias=MOD[:, sh_col + c:sh_col + c + 1],
                                                                    scale=DER[:, gs_col + c:gs_col + c + 1]),
                         [TMPB[i]] + BARRIER, [HB[c]])
                else:
                    K.op("pool", lambda e, c=c, i=i: e.tensor_tensor(TMP[i][:, :], xc(X, c), rs[:, :], ALU.mult), [XB[c], rsb], [TMPB[i]])
                    K.op("dve", lambda e, c=c, i=i: e.tensor_scalar(xc(H, c), TMP[i][:, :], DER[:, gs_col + c:gs_col + c + 1],
                                                                     MOD[:, sh_col + c:sh_col + c + 1], ALU.mult, ALU.add),
                         [TMPB[i]] + BARRIER, [HB[c]])

        def postnorm_residual(X, XB, gg_col, next_sq):
            finish_rstd()
            for c in range(8):
                i = c % 4
                K.op("dve", lambda e, c=c, i=i: e.scalar_tensor_tensor(
                    TMP[i][:, :], xc(MIXF, c), DER[:, gg_col + c:gg_col + c + 1], RS[:, :], ALU.mult, ALU.mult),
                    [MIXB[c], RSB] + BARRIER, [TMPB[i]])
                K.op("dve", lambda e, c=c, i=i: e.tensor_tensor(xc(X, c), xc(X, c), TMP[i][:, :], ALU.add), [TMPB[i], XB[c]], [XB[c]])
                if next_sq:
                    sq_from(xc(X, c), XB[c], c)
                    sumsq_mm(c)

        def proj_group(bk, bkb, unit, unitb, col0=0, ncols=T):
            for kc in range(8):
                mm(bk[:, col0:col0 + ncols], unit[:, kc * 128:(kc + 1) * 128], xc(H, kc), start=(kc == 0), stop=(kc == 7),
                   reads=[unitb, HB[kc]], writes=[bkb])
            next_unit()

        def proj_multi(specs):
            for kc in range(8):
                for (bk, bkb, unit, unitb) in specs:
                    mm(bk[:, :], unit[:, kc * 128:(kc + 1) * 128], xc(H, kc), start=(kc == 0), stop=(kc == 7),
                       reads=[unitb, HB[kc]], writes=[bkb])
            for _ in specs:
                next_unit()

        def make_tables(t):
            i = t % 2
            dma("sp", POSI[:, :], pos_d[:, t * T:(t + 1) * T], [], [POSIB], d_pos)
            POSF, TT_, KF, FR = TMP[0], TMP[1], RS, RT
            TABB = [TMPB[0], TMPB[1], RSB, RTB]
            K.op("dve", lambda e: e.tensor_copy(POSF[:, :], POSI[:, :]), [POSIB], TABB)
            for which in range(2):
                addc = 0.0 if which == 0 else 0.25
                K.op("dve", lambda e, addc=addc: e.tensor_scalar(TT_[:, :], POSF[:, :], CST[:, 0:1], addc, ALU.mult, ALU.add), TABB + [CONSTB], TABB)
                K.op("dve", lambda e: e.tensor_scalar(KF[:, :], TT_[:, :], MAGIC, MAGIC, ALU.add, ALU.subtract), TABB, TABB)
                K.op("dve", lambda e: e.tensor_tensor(FR[:, :], TT_[:, :], KF[:, :], ALU.subtract), TABB, TABB)
                if which == 0:
                    K.op("act", lambda e: e.activation(STAB[i][:, :], FR[:, :], AF.Sin, scale=CST[:, 1:2]), TABB + [CONSTB], [CSB[i]] + TABB)
                else:
                    K.op("act", lambda e: e.activation(CTAB[i][:, :], FR[:, :], AF.Sin, scale=TWO_PI_SAFE), TABB, [CSB[i]] + TABB)

        def tile_layer(t, l, pre_done=False, mid_hook=None, early_stats=None, early_h=None):
            first = (t == 0)
            X, XB = XT[t % 2], XTB[t % 2]
            ti = t % 2
            v0, d0, m0 = l * VL, l * 32, l * 48

            if not pre_done:
                if l == 0:
                    prenorm_stats(X, XB)
                prenorm_h(X, XB, d0 + 0, m0 + 0)
            qspecs = []
            for g in range(4):
                bk, bkb = bank()
                unit, unitb = W(t, l, g)
                qspecs.append((bk, bkb, unit, unitb))
            proj_multi(qspecs)
            kb, kbb = bank()
            unit, unitb = W(t, l, U_K)
            proj_group(kb, kbb, unit, unitb)
            qk_banks = [(s[0], s[1]) for s in qspecs] + [(kb, kbb)]
            for g in range(5):
                bk, bkb = qk_banks[g]
                bcol = v0 + 32 + g
                K.op("act", lambda e, bk=bk, g=g, bcol=bcol: e.activation(QB[g][:, :], bk[:, :], AF.Identity, bias=VEC[:, bcol:bcol + 1], scale=1.0),
                     [bkb, CONSTB], [QBB[g]])
            bk, bkb = bank()
            unit, unitb = W(t, l, U_V)
            for blk in range(4):
                for kc in range(8):
                    mm(bk[:, blk * 128:(blk + 1) * 128], H[:, kc * T + blk * 128: kc * T + (blk + 1) * 128], unit[:, kc * 128:(kc + 1) * 128],
                       start=(kc == 0), stop=(kc == 7), reads=[unitb, HB[kc]], writes=[bkb], signal=(kc == 7 and blk == 3))
            next_unit()
            K.op("dve", lambda e, bk=bk: e.tensor_tensor(VT[l][:, 128:640], bk[:, :], BVR[:, l * 512:(l + 1) * 512], ALU.add), [bkb, CONSTB], [VTB[l]])
            for g in range(5):
                i = g % 2
                b2, b2b = bank()
                mm(b2[:, :], PERM, QB[g][:, :], start=True, stop=True, reads=[QBB[g], CONSTB], writes=[b2b])
                K.op("dve", lambda e, b2=b2, i=i: e.tensor_tensor(TR[i][:, :], b2[:, :], STAB[ti][:, :], ALU.mult), [b2b, CSB[ti]], [TRB[i]])
                K.op("dve", lambda e, i=i, g=g: e.tensor_tensor(QC[i][:, :], QB[g][:, :], CTAB[ti][:, :], ALU.mult), [QBB[g], CSB[ti]], [QCB[i]])
                if g < 4:
                    K.op("dve", lambda e, i=i, g=g: e.tensor_tensor(xc(QT, g), QC[i][:, :], TR[i][:, :], ALU.add), [QCB[i], TRB[i]], [QTB[g]])
                else:
                    K.op("dve", lambda e, i=i: e.tensor_tensor(KT[l][0:64, 128:640], QC[i][0:64, :], TR[i][0:64, :], ALU.add),
                         [QCB[i], TRB[i]], [KTB[l]])
                    K.op("dve", lambda e, i=i: e.tensor_tensor(KT[l][64:128, 640 + 128:640 + 640], QC[i][64:128, :], TR[i][64:128, :], ALU.add),
                         [QCB[i], TRB[i]], [KTB[l]])
            for gi in range(4):
                bk, bkb = bank()
                unit, unitb = W(t, l, U_U0 + gi)
                proj_group(bk, bkb, unit, unitb)
                bcol = v0 + 37 + gi
                K.op("act", lambda e, bk=bk, gi=gi, bcol=bcol: e.activation(UT[:, gi * 528 + 16: gi * 528 + 528], bk[:, :], AF.Identity,
                                                                               bias=VEC[:, bcol:bcol + 1], scale=1.0), [bkb, CONSTB], [UTB[gi]])
            UT3 = UT[:, :].rearrange("p (g w) -> p g w", g=4)
            K.op("pool", lambda e: e.tensor_copy(UT3[:, :, 0:16], UH[l][:, :].rearrange("p (g w) -> p g w", g=4)), [UHB[l]], UTB)
            for gi in range(4):
                w = 2 << gi
                cur, curb = UT[:, gi * 528:(gi + 1) * 528], UTB[gi]
                for k in range(1, gi + 2):
                    sh = 1 << (k - 1)
                    dst, dstb = LV[k % 2], LVB[k % 2]
                    K.op("pool", lambda e, cur=cur, dst=dst, sh=sh: e.tensor_tensor(dst[:, sh:528], cur[:, sh:528], cur[:, 0:528 - sh], ALU.add),
                         [curb], [dstb])
                    cur, curb = dst[:, :], dstb
                oth, othb = LV[(gi + 2) % 2], LVB[(gi + 2) % 2]
                K.op("pool", lambda e, cur=cur, oth=oth, w=w: e.tensor_scalar(oth[:, 16:528], cur[:, 16:528], 1.0 / w, 0.0, ALU.mult, ALU.add),
                     [curb], [othb])
                K.op("pool", lambda e, oth=oth, gi=gi: e.tensor_tensor(xc(PL, gi), oth[:, 16:528], UT[:, gi * 528 + 16:gi * 528 + 528], ALU.subtract),
                     [othb, UTB[gi]], [PLB[gi]])
                if first:
                    K.op("pool", lambda e, cur=cur, gi=gi: e.tensor_tensor(T16[:, :], cur[:, 16:32], INVC[:, gi * 16:(gi + 1) * 16], ALU.mult),
                         [curb, CONSTB], [T16B])
                    K.op("pool", lambda e, gi=gi: e.tensor_tensor(PL[:, gi * T:gi * T + 16], T16[:, :], UT[:, gi * 528 + 16:gi * 528 + 32], ALU.subtract),
                         [T16B, UTB[gi]], [PLB[gi]])
            K.op("pool", lambda e: e.tensor_copy(UH[l][:, :].rearrange("p (g w) -> p g w", g=4), UT3[:, :, 512:528]), UTB, [UHB[l]])

            pr_i = [0]

            def scores(j):
                for kv in range(2):
                    for pc in range(2):
                        if first and j == 0 and pc == 0:
                            continue
                        bk, bkb = bank()
                        kcol = kv * 640 + (j + pc) * 128
                        for g in range(4):
                            mm(bk[:, g * 128:(g + 1) * 128], KT[l][:, kcol:kcol + 128], QT[:, g * T + j * 128: g * T + (j + 1) * 128],
                               start=True, stop=True, reads=[KTB[l], QTB[g]], writes=[bkb], signal=(g == 3))
                        r = pr_i[0] % 2
                        pr_i[0] += 1
                        K.op("act", lambda e, bk=bk, r=r: e.activation(PR[r][:, :], bk[:, :], AF.Exp, scale=0.125), [bkb], [PRB[r]])
                        pm = (j % 2) * 4 + kv * 2 + pc
                        moff = ((l * 2 + kv) * 2 + pc) * T
                        K.op("dve", lambda e, r=r, pm=pm, moff=moff: e.tensor_tensor(PM[pm][:, :], PR[r][:, :], MK[:, moff:moff + T], ALU.mult),
                             [PRB[r]] + BARRIER, [PMB[pm]])

            def pv(j):
                pcs = [1] if (first and j == 0) else [0, 1]
                ob, obb = bank()
                db, dbb = bank()
                for pc in pcs:
                    for kv in range(2):
                        pm = (j % 2) * 4 + kv * 2 + pc
                        last = (pc == pcs[-1] and kv == 1)
                        mm(db[kv * 64:(kv + 1) * 64, :], ONES1, PM[pm][:, :], start=(pc == pcs[0]), stop=(pc == pcs[-1]),
                           reads=[CONSTB, PMB[pm]], writes=[dbb], signal=last, tp=(0, kv * 64))
                for pc in pcs:
                    for kv in range(2):
                        pm = (j % 2) * 4 + kv * 2 + pc
                        vcol = (j + pc) * 128 + kv * 64
                        last = (pc == pcs[-1] and kv == 1)
                        mm(ob[kv * 64:(kv + 1) * 64, :], VT[l][:, vcol:vcol + 64], PM[pm][:, :], start=(pc == pcs[0]), stop=(pc == pcs[-1]),
                           reads=[VTB[l], PMB[pm]], writes=[obb], signal=last, tp=(0, kv * 64))
                K.op("act", lambda e, db=db: e.activation(RT[:, :], db[:, :], AF.Ln, bias=CST[:, 4:5], scale=1.0), [dbb, CONSTB], [RTB])
                K.op("act", lambda e: e.activation(RT[:, :], RT[:, :], AF.Exp, scale=-1.0), [RTB], [RTB])
                AT3 = AT[:, :].rearrange("p (g t) -> p g t", g=4)
                K.op("dve", lambda e, ob=ob, j=j: e.tensor_tensor(AT3[:, :, j * 128:(j + 1) * 128],
                                                                   ob[:, :].rearrange("p (g q) -> p g q", g=4),
                                                                   RT[:, :].rearrange("p (g q) -> p g q", g=4), ALU.mult), [obb, RTB], ATB)

            scores(0)
            for j in range(4):
                if j + 1 < 4:
                    scores(j + 1)
                pv(j)
            punit, punitb = W(t, l, U_POOL)
            for gi in range(4):
                bk, bkb = bank()
                mm(bk[:, :], punit[:, gi * 128:(gi + 1) * 128], xc(PL, gi), start=True, stop=True, reads=[punitb, PLB[gi]], writes=[bkb])
                if gi == 3:
                    next_unit()
                scol = v0 + 41 + gi
                K.op("act", lambda e, bk=bk, gi=gi, scol=scol: e.activation(xc(PO, gi), bk[:, :], AF.Identity, scale=VEC[:, scol:scol + 1]),
                     [bkb, CONSTB], [POB[gi]])
            KT3 = KT[l][:, :].rearrange("p (m w) -> p m w", m=2)
            K.op("pool", lambda e: e.tensor_copy(KT3[:, :, 0:128], KT3[:, :, 512:640]), [KTB[l]], [KTB[l]])
            K.op("pool", lambda e: e.tensor_copy(VT[l][:, 0:128], VT[l][:, 512:640]), [VTB[l]], [VTB[l]])

            for o in range(8):
                bk, bkb = bank()
                unit, unitb = W(t, l, U_WO0 + o)
                for kc in range(8):
                    rhs, rb = (xc(AT, kc), ATB[kc]) if kc < 4 else (xc(PO, kc - 4), POB[kc - 4])
                    mm(bk[:, :], unit[:, kc * 128:(kc + 1) * 128], rhs, start=(kc == 0), stop=(kc == 7), reads=[unitb, rb], writes=[bkb])
                next_unit()
                K.op("act", lambda e, bk=bk, o=o: e.activation(xc(MIXF, o), bk[:, :], AF.Identity), [bkb], [MIXB[o]])
                sq_from(bk[:, :], bkb, o)
                if o >= 1:
                    sumsq_mm(o - 1)
            sumsq_mm(7)
            postnorm_residual(X, XB, d0 + 8, next_sq=True)

            prenorm_h(X, XB, d0 + 16, m0 + 24)

            def ffn_evac(f, bg, bgb, bu, bub):
                i = f % 2
                K.op("act", lambda e: e.activation(SG[i][:, :], bg[:, :], AF.Silu), [bgb], [SGB[i]])
                K.op("dve", lambda e: e.tensor_tensor(xc(ACTB, f), bu[:, :], SG[i][:, :], ALU.mult), [bub, SGB[i]], [ACTBB[f]])

            NF0 = 3
            specs = []
            for f in range(NF0):
                for k2 in range(2):
                    bk, bkb = bank()
                    unit, unitb = W(t, l, U_G0 + 2 * f + k2)
                    specs.append((bk, bkb, unit, unitb))
            proj_multi(specs)
            for f in range(NF0):
                ffn_evac(f, specs[2 * f][0], specs[2 * f][1], specs[2 * f + 1][0], specs[2 * f + 1][1])
            for f in range(NF0, NF):
                bg, bgb = bank()
                unit, unitb = W(t, l, U_G0 + 2 * f)
                proj_group(bg, bgb, unit, unitb)
                bu, bub = bank()
                unit, unitb = W(t, l, U_G0 + 2 * f + 1)
                proj_group(bu, bub, unit, unitb)
                ffn_evac(f, bg, bgb, bu, bub)
                if f == 10 and mid_hook is not None:
                    mid_hook()
            for o in range(8):
                bk, bkb = bank()
                s = wslot_of[(t, l, o)]
                for f in range(NF):
                    mm(bk[:, :], WDR[s][:, f * 128:(f + 1) * 128], xc(ACTB, f), start=(f == 0), stop=(f == NF - 1),
                       reads=[WDRB[s], ACTBB[f]], writes=[bkb])
                next_wd()
                K.op("act", lambda e, bk=bk, o=o: e.activation(xc(MIXF, o), bk[:, :], AF.Identity), [bkb], [MIXB[o]])
                sq_from(bk[:, :], bkb, o)
                if o >= 1:
                    sumsq_mm(o - 1)
                if o == 3 and early_stats is not None:
                    early_stats()
            sumsq_mm(7)
            if early_h is not None:
                early_h()
            postnorm_residual(X, XB, d0 + 24, next_sq=(l + 1 < L))

        dma("sp", XT[0][:, :], xT_d[0, :, :], [], XTB[0], d_x[0])
        for _ in range(NSLOT):
            next_unit()
        for _ in range(NWD):
            next_wd()
        make_tables(0)
        if NT > 1:
            dma("sp", XT[1][:, :], xT_d[1, :, :], [], XTB[1], d_x[1])
        PMS = (PM[0:4], PMB[0:4])
        for t in range(NT):
            nxt = t + 1 < NT
            Xn, XnB = XT[(t + 1) % 2], XTB[(t + 1) % 2]

            def mid_hook(t=t):
                if t + 1 < NT:
                    if t >= 1:
                        i = (t + 1) % 2
                        dma("sp", XT[i][:, :], xT_d[t + 1, :, :], [], XTB[i], d_x[i])
                    make_tables(t + 1)

            def early_stats(Xn=Xn, XnB=XnB):
                prenorm_stats(Xn, XnB, NRM2, NRM2B, PMS)

            def early_h(Xn=Xn, XnB=XnB):
                prenorm_h(Xn, XnB, 0, 0, NRM2, NRM2B, RT, RTB)

            for l in range(L):
                last = (l == L - 1)
                tile_layer(t, l, pre_done=(l == 0 and t > 0), mid_hook=(mid_hook if l == 0 else None),
                           early_stats=(early_stats if (last and nxt) else None), early_h=(early_h if (last and nxt) else None))
            dma("pool", out_d[t, :, :], XT[t % 2][:, :], XTB[t % 2], [OUTB], d_out)
        K.streams["pool"].ops.append(lambda e: e.wait_ge(d_out.h, d_out.count))
        if debug:
            K.streams["act"].ops.append(lambda e: e.wait_ge(d_dbg.h, d_dbg.count))

        with nc.Block() as block:
            @block.sync
            def _(e):
                for f in K.streams["sp"].ops:
                    f(e)

            @block.tensor
            def _(e):
                for f in K.streams["pe"].ops:
                    f(e)

            @block.scalar
            def _(e):
                for f in K.streams["act"].ops:
                    f(e)

            @block.vector
            def _(e):
                for f in K.streams["dve"].ops:
                    f(e)

            @block.gpsimd
            def _(e):
                for f in K.streams["pool"].ops:
                    f(e)
    return nc


def _unit(Wm, rows, cols):
    sub = Wm[np.asarray(rows)][:, np.asarray(cols)]
    return sub.reshape(8, 128, 128).transpose(1, 0, 2)


def pack_shared(inp, L=2):
    f32 = np.float32
    ada_w, ada_b = np.asarray(inp["ada_w"], f32), np.asarray(inp["ada_b"], f32)
    w_in, b_in = np.asarray(inp["w_in"], f32), np.asarray(inp["b_in"], f32)
    sinks = np.asarray(inp["sinks"], f32)
    pool_w, pool_scale = np.asarray(inp["pool_w"], f32), np.asarray(inp["pool_scale"], f32)
    w_out = np.asarray(inp["w_out"], f32)
    w_gate, w_up, w_down = np.asarray(inp["w_gate"], f32), np.asarray(inp["w_up"], f32), np.asarray(inp["w_down"], f32)
    gs = [np.asarray(inp[k], f32) for k in ("g_pre_mix", "g_post_mix", "g_pre_ffn", "g_post_ffn")]

    adaw = np.empty((L * 24, 128, 2, 8, 128), f32)
    for l in range(L):
        adaw[l * 24:(l + 1) * 24] = ada_w[l].reshape(8, 128, 24, 2, 128).transpose(2, 1, 3, 0, 4)
    adaw = adaw.reshape(L * 24, 128, 2048)
    adab = np.concatenate([ada_b[l].reshape(48, 128).T for l in range(L)], axis=1)

    nat = np.arange(1024)
    qcols = [np.concatenate([np.arange(g * 64, g * 64 + 64), np.arange((g + 4) * 64, (g + 4) * 64 + 64)]) for g in range(4)]
    mixrows = np.concatenate([qcols[g] for g in range(4)] + [np.arange(512, 1024)])
    wu = np.zeros((L, NU, 128, 8, 128), f32)
    wd = np.empty((L, 8, 128, NF, 128), f32)
    for l in range(L):
        for g in range(4):
            wu[l, g] = _unit(w_in[l], nat, qcols[g])
        wu[l, U_K] = _unit(w_in[l], nat, np.arange(512, 640))
        wu[l, U_V] = _unit(w_in[l], nat, np.arange(640, 768))
        for gi in range(4):
            wu[l, U_U0 + gi] = _unit(w_in[l], nat, np.arange(768 + gi * 128, 768 + (gi + 1) * 128))
            wu[l, U_POOL, :, gi, :] = pool_w[l, gi]
        for o in range(8):
            wu[l, U_WO0 + o] = _unit(w_out[l], mixrows, np.arange(o * 128, (o + 1) * 128))
            wd[l, o] = w_down[l][:, o * 128:(o + 1) * 128].reshape(NF, 128, 128).transpose(1, 0, 2)
        for f in range(NF):
            wu[l, U_G0 + 2 * f] = _unit(w_gate[l], nat, np.arange(f * 128, (f + 1) * 128))
            wu[l, U_G0 + 2 * f + 1] = _unit(w_up[l], nat, np.arange(f * 128, (f + 1) * 128))
    wu = wu.reshape(-1, 16384)
    wd = wd.reshape(-1, 11264)

    def pcol(v):
        return v.reshape(-1, 128).T

    vecs = np.empty((128, L * VL), f32)
    bvrep = np.empty((128, L * 512), f32)
    sinkrep = np.empty((128, L * 8), f32)
    for l in range(L):
        v0 = l * VL
        for i in range(4):
            vecs[:, v0 + 8 * i:v0 + 8 * i + 8] = pcol(gs[i][l])
        for g in range(4):
            vecs[:, v0 + 32 + g] = b_in[l][qcols[g]]
        vecs[:, v0 + 36] = b_in[l][512:640]
        vecs[:, v0 + 37:v0 + 41] = pcol(b_in[l][768:1280])
        vecs[:, v0 + 41:v0 + 45] = pcol(pool_scale[l])
        bvrep[:, l * 512:(l + 1) * 512] = np.tile(b_in[l][640:768], (128, 4))
        sinkrep[:, l * 8:(l + 1) * 8] = np.tile(sinks[l], (128, 1))

    cst = np.zeros((128, 8), f32)
    inv_freq = (ROPE_THETA ** (-np.arange(0, 16, 2, dtype=np.float32) / 16)).astype(f32)
    for p in range(128):
        r = p % 64
        if r < 16:
            cst[p, 0] = np.float32(inv_freq[r % 8]) / np.float32(2 * np.pi)
            cst[p, 1] = (-1.0 if r < 8 else 1.0) * TWO_PI_SAFE
    cst[:, 2] = TWO_PI_SAFE
    cst[:, 3] = EPS
    cst[:, 4] = 1.0
    invc = np.zeros((128, 64), f32)
    for gi in range(4):
        w = 2 << gi
        invc[:, gi * 16:(gi + 1) * 16] = 1.0 / np.minimum(np.arange(16) + 1.0, float(w))
    cmat = np.zeros((128, 576), f32)
    cmat[:, 0:128] = 1.0 / 1024.0
    cmat[:, 128:192] = 1.0
    for m in range(128):
        r = m % 64
        if r < 8:
            cmat[m + 8, 192 + m] = 1.0
        elif r < 16:
            cmat[m - 8, 192 + m] = 1.0
    s_idx = np.arange(128)[:, None]
    q_idx = np.arange(128)[None, :]
    cmat[:, 320:448] = (s_idx > q_idx).astype(f32)
    cmat[:, 448:576] = (s_idx <= q_idx).astype(f32)
    return dict(adaw=adaw, adab=np.ascontiguousarray(adab), wu=wu, wd=wd, vecs=vecs, bvrep=bvrep, sinkrep=sinkrep,
                cst=cst, invc=invc, cmat=cmat)


def pack_core(x_b, c_b, pos_b):
    S = x_b.shape[0]
    NT = S // T
    xT = np.ascontiguousarray(np.asarray(x_b, np.float32).reshape(NT, T, 8, 128).transpose(0, 3, 2, 1)).reshape(NT, 128, 8 * T)
    pos = np.ascontiguousarray(np.broadcast_to(np.asarray(pos_b, np.int32)[None, :], (128, S)))
    cT = np.ascontiguousarray(np.asarray(c_b, np.float32).reshape(8, 128).T)
    return dict(xT=xT, pos=pos, cT=cT)


def unpack_out(outT, S):
    NT = S // T
    return outT.reshape(NT, 128, 8, T).transpose(0, 3, 2, 1).reshape(S, D)


_CACHE = {}


def kernel(_debug=False, **inputs):
    x = np.asarray(inputs["x"])
    B, S, _ = x.shape
    shared = pack_shared(inputs)
    in_maps = []
    for b in range(B):
        m = dict(shared)
        m.update(pack_core(x[b], np.asarray(inputs["c"])[b], np.asarray(inputs["positions"])[b]))
        in_maps.append(m)
    key = (S, _debug)
    if key not in _CACHE:
        _CACHE[key] = build_program(S, debug=_debug)
    nc = _CACHE[key]
    res = run_bass_kernel_spmd(nc, in_maps, core_ids=list(range(B)))
    if _debug:
        kernel.dbg = [np.where(np.arange(24)[:, None, None] >= 0, 0, 0) for b in range(0)]
        for b in range(B):
            d32 = np.asarray(res.results[b]["dbg"]).astype(np.float32)
            d16 = np.asarray(res.results[b]["dbgh"]).astype(np.float32)
            halfidx = [2, 3, 4, 5, 6, 8, 9, 15, 16, 10, 13]
            for i in halfidx:
                d32[i] = d16[i]
            kernel.dbg.append(d32)
    out = np.empty((B, S, D), np.float32)
    for b in range(B):
        out[b] = unpack_out(np.asarray(res.results[b]["outT"]), S)
    return out
```

```python
import numpy as np
from contextlib import ExitStack
import concourse.bass as bass
import concourse.mybir as mybir
from concourse.bass_utils import run_bass_kernel_spmd

F32 = mybir.dt.float32
BF16 = mybir.dt.bfloat16
I32 = mybir.dt.int32
ALU = mybir.AluOpType
AF = mybir.ActivationFunctionType

D = 1024
T = 512
DFF = 2816
NF = 22
NU = 63
VL = 45
NSLOT = 8
NWD = 3
EPS = 1e-6
MAGIC = 12582912.0
TWO_PI_SAFE = 6.28318
ROPE_THETA = 500000.0

U_K, U_V, U_U0, U_POOL, U_WO0, U_G0 = 4, 5, 6, 10, 11, 19


class Buf:
    __slots__ = ("w", "r", "const", "name")

    def __init__(self, name="", const=False):
        self.w = None
        self.r = {}
        self.const = const
        self.name = name


class DmaSem:
    def __init__(self, h):
        self.h = h
        self.count = 0


class Stream:
    def __init__(self, name, sem):
        self.name = name
        self.sem = sem
        self.count = 0
        self.ops = []
        self.seen = {}


class Tracker:
    def __init__(self):
        self.streams = {}

    def op(self, en, fn, reads=(), writes=(), signal=True, dma=None):
        st = self.streams[en]
        deps = []
        for b in reads:
            if b.w is not None:
                deps.append(b.w)
        for b in writes:
            if b.w is not None:
                deps.append(b.w)
            deps.extend(b.r.values())
        waits = {}
        for (sem, val) in deps:
            if en == "pe" and sem is st.sem:
                continue
            key = id(sem)
            if st.seen.get(key, 0) >= val:
                continue
            if key not in waits or waits[key][1] < val:
                waits[key] = (sem, val)
        for sem, val in waits.values():
            st.seen[id(sem)] = val
            st.ops.append(lambda e, sem=sem, val=val: e.wait_ge(sem, val))
        if dma is not None:
            dma.count += 16
            ev = (dma.h, dma.count)
            st.ops.append(lambda e, fn=fn, s=dma.h: fn(e).then_inc(s, 16))
        elif signal:
            st.count += 1
            ev = (st.sem, st.count)
            st.ops.append(lambda e, fn=fn, s=st.sem: fn(e).then_inc(s, 1))
        else:
            ev = (st.sem, st.count + 1)
            st.ops.append(lambda e, fn=fn: fn(e))
        for b in writes:
            b.w = ev
            b.r = {}
        for b in reads:
            if not b.const:
                k = id(ev[0])
                if k not in b.r or b.r[k][1] < ev[1]:
                    b.r[k] = ev
        return ev


def build_program(S, L=2, debug=False):
    NT = S // T
    nc = bass.Bass("TRN2", target_bir_lowering=False)

    def din(name, shape, dt=F32):
        return nc.dram_tensor(name, shape, dt, kind="ExternalInput").ap()

    xT_d = din("xT", [NT, 128, 8 * T])
    pos_d = din("pos", [128, S], I32)
    cT_d = din("cT", [128, 8])
    adaw_d = din("adaw", [L * 24, 128, 2048])
    adab_d = din("adab", [128, L * 48])
    wu_d = din("wu", [L * NU * 128 * 1024 // 16384, 16384])
    wd_d = din("wd", [L * 8 * 32, 11264])
    vec_d = din("vecs", [128, L * VL])
    bv_d = din("bvrep", [128, L * 512])
    snk_d = din("sinkrep", [128, L * 8])
    cst_d = din("cst", [128, 8])
    invc_d = din("invc", [128, 64])
    cmat_d = din("cmat", [128, 576])
    out_d = nc.dram_tensor("outT", [NT, 128, 8 * T], F32, kind="ExternalOutput").ap()
    dbg_d = nc.dram_tensor("dbg", [24, 128, 4096], F32, kind="ExternalOutput").ap() if debug else None
    dbgh_d = nc.dram_tensor("dbgh", [24, 128, 4096], BF16, kind="ExternalOutput").ap() if debug else None
    wub_d = nc.dram_tensor("wub", [L * NU * 128 * 1024 // 16384, 16384], BF16, kind="Internal").ap()
    wdb_d = nc.dram_tensor("wdb", [L * 8 * 32, 11264], BF16, kind="Internal").ap()

    K = Tracker()
    with ExitStack() as es:
        def sb(name, cols, dt=F32):
            return es.enter_context(nc.sbuf_tensor(name, [128, cols], dt))

        def sem(name):
            return es.enter_context(nc.semaphore(name))

        XT = [sb(f"xt{i}", 8 * T) for i in range(2)]
        XTB = [[Buf(f"x{i}_{c}") for c in range(8)] for i in range(2)]
        H = sb("h", 8 * T, BF16)
        HB = [Buf(f"h{c}") for c in range(8)]
        SQ = [sb(f"sq{i}", T, BF16) for i in range(4)]
        SQB = [Buf() for _ in range(4)]
        RS = sb("rs", T)
        RSB = Buf("rs")
        TMP = [sb(f"tmp{i}", T) for i in range(4)]
        TMPB = [Buf() for _ in range(4)]
        QB = [sb(f"qb{i}", T, BF16) for i in range(5)]
        QBB = [Buf() for _ in range(5)]
        QC = [sb(f"qc{i}", T, BF16) for i in range(2)]
        QCB = [Buf() for _ in range(2)]
        TR = [sb(f"tr{i}", T, BF16) for i in range(2)]
        TRB = [Buf() for _ in range(2)]
        QT = sb("qt", 4 * T, BF16)
        QTB = [Buf(f"qt{g}") for g in range(4)]
        KT = [sb(f"kt{l}", 2 * 640, BF16) for l in range(L)]
        KTB = [Buf(f"kt{l}") for l in range(L)]
        VT = [sb(f"vt{l}", 640, BF16) for l in range(L)]
        VTB = [Buf(f"vt{l}") for l in range(L)]
        UT = sb("ut", 4 * 528)
        UTB = [Buf(f"ut{g}") for g in range(4)]
        UH = [sb(f"uh{l}", 64) for l in range(L)]
        UHB = [Buf(f"uh{l}") for l in range(L)]
        LV = [sb(f"lv{i}", 528) for i in range(2)]
        LVB = [Buf() for _ in range(2)]
        T16 = sb("t16", 16)
        T16B = Buf()
        PL = sb("pl", 4 * T, BF16)
        PLB = [Buf() for _ in range(4)]
        PO = sb("po", 4 * T, BF16)
        POB = [Buf() for _ in range(4)]
        AT = sb("at", 4 * T, BF16)
        ATB = [Buf() for _ in range(4)]
        PR = [sb(f"pr{i}", T, BF16) for i in range(2)]
        PRB = [Buf() for _ in range(2)]
        PM = [sb(f"pm{i}", T, BF16) for i in range(8)]
        PMB = [Buf() for _ in range(8)]
        RT = sb("rt", T)
        RTB = Buf()
        MIXF = sb("mixf", 8 * T)
        MIXB = [Buf(f"mix{c}") for c in range(8)]
        ACTB = sb("actb", NF * T, BF16)
        ACTBB = [Buf() for _ in range(NF)]
        SG = [sb(f"sg{i}", T, BF16) for i in range(2)]
        SGB = [Buf() for _ in range(2)]
        CTAB = [sb(f"ctab{i}", T, BF16) for i in range(2)]
        STAB = [sb(f"stab{i}", T, BF16) for i in range(2)]
        CSB = [Buf() for _ in range(2)]
        POSI = sb("posi", T, I32)
        POSIB = Buf()
        MK = sb("mk", L * 4 * T, BF16)
        MKB = Buf(const=False)
        RING = [sb(f"ring{i}", 1024, BF16) for i in range(NSLOT)]
        RINGB = [Buf(f"ring{i}") for i in range(NSLOT)]
        WDR = [sb(f"wdr{i}", DFF, BF16) for i in range(NWD)]
        WDRB = [Buf(f"wdr{i}") for i in range(NWD)]
        CT = sb("ct", 8)
        DUM = sb("dum", 8)
        DUMB = Buf("dum")
        CA = sb("ca", 8)
        CAB = Buf()
        ADAB = sb("adab_s", L * 48)
        MOD = sb("mod", L * 48)
        MODB = Buf()
        DER = sb("der", L * 32)
        DERB = Buf()
        VEC = sb("vec", L * VL)
        BVR = sb("bvr", L * 512)
        SNK = sb("snk", L * 8)
        ES = sb("es", L * 8)
        CST = sb("cst_s", 8)
        INVC = sb("invc_s", 64)
        CMATF = MIXF[:, 0:576]
        CMAT = sb("cmat_s", 576, BF16)
        CONSTB = Buf("const")
        ONESM = CMAT[:, 0:128]
        ONES1 = CMAT[:, 128:192]
        PERM = CMAT[:, 192:320]

        PS = [es.enter_context(nc.psum_tensor(f"ps{i}", [128, 512], F32)) for i in range(8)]
        PSB = [Buf(f"ps{i}") for i in range(8)]

        s_pe, s_act, s_dve, s_pool, s_sp = sem("s_pe"), sem("s_act"), sem("s_dve"), sem("s_pool"), sem("s_sp")
        K.streams = {
            "pe": Stream("pe", s_pe), "act": Stream("act", s_act), "dve": Stream("dve", s_dve),
            "pool": Stream("pool", s_pool), "sp": Stream("sp", s_sp),
        }
        d_x = [DmaSem(sem(f"d_x{i}")) for i in range(2)]
        d_pos = DmaSem(sem("d_pos"))
        d_ring = [DmaSem(sem(f"d_ring{i}")) for i in range(NSLOT)]
        d_wd = [DmaSem(sem(f"d_wd{i}")) for i in range(NWD)]
        d_const = DmaSem(sem("d_const"))
        d_aw = [DmaSem(sem(f"d_aw{i}")) for i in range(2)]
        d_cast = DmaSem(sem("d_cast"))
        d_out = DmaSem(sem("d_out"))
        CASTB = Buf("cast")
        OUTB = Buf("outdram")

        bank_rr = [0]

        def bank():
            i = bank_rr[0] % 6
            bank_rr[0] += 1
            return PS[i], PSB[i]

        NRM, NRMB = PS[7], PSB[7]
        NRM2, NRM2B = PS[6], PSB[6]

        def mm(out, lhsT, rhs, start, stop, reads, writes, signal=None, tp=None):
            if signal is None:
                signal = stop
            if tp is None:
                fn = lambda e: e.matmul(out, lhsT, rhs, start=start, stop=stop)
            else:
                fn = lambda e: e.matmul(out, lhsT, rhs, start=start, stop=stop, tile_position=tp)
            K.op("pe", fn, reads, writes, signal)

        def dma(en, out, in_, reads, writes, dsem):
            K.op(en, lambda e: e.dma_start(out=out, in_=in_), reads, writes, dma=dsem)

        d_dbg = DmaSem(sem("d_dbg"))
        DBGB = Buf("dbg")

        def dump(idx, ap, cols, bufs, half=False):
            if debug:
                dst = dbgh_d if half else dbg_d
                dma("act", dst[idx, :, 0:cols], ap, list(bufs), [DBGB], d_dbg)

        nrow_u = L * NU * 128 * 1024 // 16384
        nrow_d = L * 8 * 32
        step = 63
        for r0 in range(0, nrow_u, step):
            r1 = min(nrow_u, r0 + step)
            dma("pool", wub_d[r0:r1, :], wu_d[r0:r1, :], [], [], d_cast)
        step = 64
        for r0 in range(0, nrow_d, step):
            r1 = min(nrow_d, r0 + step)
            dma("pool", wdb_d[r0:r1, :], wd_d[r0:r1, :], [], [], d_cast)
        CASTB.w = (d_cast.h, d_cast.count)

        for (dst, src) in ((CT, cT_d), (ADAB, adab_d), (VEC, vec_d), (BVR, bv_d), (SNK, snk_d), (CST, cst_d),
                           (INVC, invc_d)):
            dma("sp", dst[:, :], src[:, :], [], [CONSTB], d_const)
        dma("sp", MIXF[:, 0:576], cmat_d[:, :], [], [CONSTB, MIXB[0], MIXB[1]], d_const)
        K.op("dve", lambda e: e.tensor_copy(CMAT[:, :], MIXF[:, 0:576]), [CONSTB, MIXB[0], MIXB[1]], [CONSTB])
        CONSTB.const = False
        K.op("pool", lambda e: e.memset(UT[:, :], 0.0), [], UTB)
        for l in range(L):
            K.op("pool", lambda e, l=l: e.memset(KT[l][:, :], 0.0), [], [KTB[l]])
            K.op("pool", lambda e, l=l: e.memset(VT[l][:, :], 0.0), [], [VTB[l]])
            K.op("pool", lambda e, l=l: e.memset(UH[l][:, :], 0.0), [], [UHB[l]])
        for i in range(2):
            K.op("pool", lambda e, i=i: e.memset(LV[i][:, :], 0.0), [], [LVB[i]])
        K.op("act", lambda e: e.activation(CA[:, :], CT[:, :], AF.Silu), [CONSTB], [CAB])
        K.op("act", lambda e: e.activation(ES[:, :], SNK[:, :], AF.Exp, scale=-1.0), [CONSTB], [CONSTB])
        for l in range(L):
            for kv in range(2):
                for pc in range(2):
                    off = ((l * 2 + kv) * 2 + pc) * T
                    for g in range(4):
                        col = l * 8 + kv * 4 + g
                        K.op("dve", lambda e, off=off, g=g, pc=pc, col=col: e.tensor_scalar(
                            MK[:, off + g * 128: off + (g + 1) * 128], MIXF[:, 320 + pc * 128: 320 + (pc + 1) * 128],
                            ES[:, col:col + 1], 0.0, ALU.mult, ALU.add), [CONSTB, MIXB[0], MIXB[1]], [MKB])
        modbank, modbankb = bank()
        for l in range(L):
            for pc in range(24):
                i = pc % 2
                stg = MIXF[:, i * 2048:(i + 1) * 2048]
                stgb = MIXB[i * 4:(i + 1) * 4]
                dma("sp", stg, adaw_d[l * 24 + pc, :, :], [], stgb, d_aw[i])
                for o in range(2):
                    col = l * 48 + pc * 2 + o
                    for kc in range(8):
                        mm(modbank[:, col:col + 1], stg[:, (o * 8 + kc) * 128:(o * 8 + kc + 1) * 128], CA[:, kc:kc + 1],
                           start=(kc == 0), stop=(kc == 7), reads=stgb + [CAB], writes=[modbankb])
        K.op("dve", lambda e: e.tensor_tensor(MOD[:, :], modbank[:, 0:L * 48], ADAB[:, :], ALU.add),
             [modbankb, CONSTB], [MODB])
        for l in range(L):
            m0 = l * 48
            v0 = l * VL
            d0 = l * 32
            K.op("dve", lambda e, m0=m0, v0=v0, d0=d0: e.scalar_tensor_tensor(
                DER[:, d0:d0 + 8], MOD[:, m0 + 8:m0 + 16], 1.0, VEC[:, v0:v0 + 8], ALU.add, ALU.mult), [MODB, CONSTB], [DERB])
            K.op("dve", lambda e, m0=m0, v0=v0, d0=d0: e.tensor_tensor(
                DER[:, d0 + 8:d0 + 16], MOD[:, m0 + 16:m0 + 24], VEC[:, v0 + 8:v0 + 16], ALU.mult), [MODB, CONSTB], [DERB])
            K.op("dve", lambda e, m0=m0, v0=v0, d0=d0: e.scalar_tensor_tensor(
                DER[:, d0 + 16:d0 + 24], MOD[:, m0 + 32:m0 + 40], 1.0, VEC[:, v0 + 16:v0 + 24], ALU.add, ALU.mult), [MODB, CONSTB], [DERB])
            K.op("dve", lambda e, m0=m0, v0=v0, d0=d0: e.tensor_tensor(
                DER[:, d0 + 24:d0 + 32], MOD[:, m0 + 40:m0 + 48], VEC[:, v0 + 24:v0 + 32], ALU.mult), [MODB, CONSTB], [DERB])
        BARRIER = [MKB, CONSTB, MODB, DERB]
        dump(0, MOD[:, :], L * 48, [MODB])
        dump(1, DER[:, :], L * 32, [DERB])
        dump(2, MK[:, :], L * 4 * T, [MKB], half=True)

        ucount = [0]
        wcount = [0]
        slot_of = {}
        wslot_of = {}

        def load_unit(t, l, u):
            s = ucount[0] % NSLOT
            ucount[0] += 1
            slot_of[(t, l, u)] = s
            r0 = (l * NU + u) * 8
            src = wub_d[r0:r0 + 8, :].rearrange("r (q c) -> (r q) c", q=16)
            dma("sp", RING[s][:, :], src, [CASTB], [RINGB[s]], d_ring[s])

        def load_wd(t, l, o):
            s = wcount[0] % NWD
            wcount[0] += 1
            wslot_of[(t, l, o)] = s
            r0 = (l * 8 + o) * 32
            src = wdb_d[r0:r0 + 32, :].rearrange("r (q c) -> (r q) c", c=DFF)
            dma("sp", WDR[s][:, :], src, [CASTB], [WDRB[s]], d_wd[s])

        ULIST = [(t_, l_, u_) for t_ in range(NT) for l_ in range(L) for u_ in range(NU)]
        WLIST = [(t_, l_, o_) for t_ in range(NT) for l_ in range(L) for o_ in range(8)]
        upos = [0]
        wpos = [0]

        def next_unit():
            if upos[0] < len(ULIST):
                load_unit(*ULIST[upos[0]])
                upos[0] += 1

        def next_wd():
            if wpos[0] < len(WLIST):
                load_wd(*WLIST[wpos[0]])
                wpos[0] += 1

        def W(t, l, u):
            s = slot_of[(t, l, u)]
            return RING[s], RINGB[s]

        def xc(X, c):
            return X[:, c * T:(c + 1) * T]

        def finish_rstd(nrm=None, nrmb=None, rs=None, rsb=None):
            nrm = NRM if nrm is None else nrm
            nrmb = NRMB if nrmb is None else nrmb
            rs = RS if rs is None else rs
            rsb = RSB if rsb is None else rsb
            K.op("act", lambda e: e.activation(rs[:, :], nrm[:, :], AF.Ln, bias=CST[:, 3:4], scale=1.0), [nrmb, CONSTB], [rsb])
            K.op("act", lambda e: e.activation(rs[:, :], rs[:, :], AF.Exp, scale=-0.5), [rsb], [rsb])

        def sq_from(src_ap, src_buf, idx, sqs=None):
            sq, sqb = (SQ, SQB) if sqs is None else sqs
            i = idx % 4
            K.op("act", lambda e: e.activation(sq[i][:, :], src_ap, AF.Square), [src_buf], [sqb[i]])

        def sumsq_mm(idx, nrm=None, nrmb=None, sqs=None):
            nrm = NRM if nrm is None else nrm
            nrmb = NRMB if nrmb is None else nrmb
            sq, sqb = (SQ, SQB) if sqs is None else sqs
            i = idx % 4
            mm(nrm[:, :], ONESM, sq[i][:, :], start=(idx == 0), stop=(idx == 7), reads=[sqb[i], CONSTB], writes=[nrmb], signal=True)

        def prenorm_stats(X, XB, nrm=None, nrmb=None, sqs=None):
            for c in range(8):
                sq_from(xc(X, c), XB[c], c, sqs)
                sumsq_mm(c, nrm, nrmb, sqs)

        def prenorm_h(X, XB, gs_col, sh_col, nrm=None, nrmb=None, rs=None, rsb=None):
            finish_rstd(nrm, nrmb, rs, rsb)
            rs = RS if rs is None else rs
            rsb = RSB if rsb is None else rsb
            for c in range(8):
                i = c % 4
                if c % 2 == 0:
                    K.op("dve", lambda e, c=c, i=i: e.tensor_tensor(TMP[i][:, :], xc(X, c), rs[:, :], ALU.mult), [XB[c], rsb], [TMPB[i]])
                    K.op("act", lambda e, c=c, i=i: e.activation(xc(H, c), TMP[i][:, :], AF.Identity,
                                                                    bias=MOD[:, sh_col + c:sh_col + c + 1],
                                                                    scale=DER[:, gs_col + c:gs_col + c + 1]),
                         [TMPB[i]] + BARRIER, [HB[c]])
                else:
                    K.op("pool", lambda e, c=c, i=i: e.tensor_tensor(TMP[i][:, :], xc(X, c), rs[:, :], ALU.mult), [XB[c], rsb], [TMPB[i]])
                    K.op("dve", lambda e, c=c, i=i: e.tensor_scalar(xc(H, c), TMP[i][:, :], DER[:, gs_col + c:gs_col + c + 1],
                                                                     MOD[:, sh_col + c:sh_col + c + 1], ALU.mult, ALU.add),
                         [TMPB[i]] + BARRIER, [HB[c]])

        def postnorm_residual(X, XB, gg_col, next_sq):
            finish_rstd()
            for c in range(8):
                i = c % 4
                K.op("dve", lambda e, c=c, i=i: e.scalar_tensor_tensor(
                    TMP[i][:, :], xc(MIXF, c), DER[:, gg_col + c:gg_col + c + 1], RS[:, :], ALU.mult, ALU.mult),
                    [MIXB[c], RSB] + BARRIER, [TMPB[i]])
                K.op("dve", lambda e, c=c, i=i: e.tensor_tensor(xc(X, c), xc(X, c), TMP[i][:, :], ALU.add), [TMPB[i], XB[c]], [XB[c]])
                if next_sq:
                    sq_from(xc(X, c), XB[c], c)
                    sumsq_mm(c)

        def proj_group(bk, bkb, unit, unitb, col0=0, ncols=T):
            for kc in range(8):
                mm(bk[:, col0:col0 + ncols], unit[:, kc * 128:(kc + 1) * 128], xc(H, kc), start=(kc == 0), stop=(kc == 7),
                   reads=[unitb, HB[kc]], writes=[bkb])
            next_unit()

        def proj_multi(specs):
            for kc in range(8):
                for (bk, bkb, unit, unitb) in specs:
                    mm(bk[:, :], unit[:, kc * 128:(kc + 1) * 128], xc(H, kc), start=(kc == 0), stop=(kc == 7),
                       reads=[unitb, HB[kc]], writes=[bkb])
            for _ in specs:
                next_unit()

        def make_tables(t):
            i = t % 2
            dma("sp", POSI[:, :], pos_d[:, t * T:(t + 1) * T], [], [POSIB], d_pos)
            POSF, TT_, KF, FR = TMP[0], TMP[1], RS, RT
            TABB = [TMPB[0], TMPB[1], RSB, RTB]
            K.op("dve", lambda e: e.tensor_copy(POSF[:, :], POSI[:, :]), [POSIB], TABB)
            for which in range(2):
                addc = 0.0 if which == 0 else 0.25
                K.op("dve", lambda e, addc=addc: e.tensor_scalar(TT_[:, :], POSF[:, :], CST[:, 0:1], addc, ALU.mult, ALU.add), TABB + [CONSTB], TABB)
                K.op("dve", lambda e: e.tensor_scalar(KF[:, :], TT_[:, :], MAGIC, MAGIC, ALU.add, ALU.subtract), TABB, TABB)
                K.op("dve", lambda e: e.tensor_tensor(FR[:, :], TT_[:, :], KF[:, :], ALU.subtract), TABB, TABB)
                if which == 0:
                    K.op("act", lambda e: e.activation(STAB[i][:, :], FR[:, :], AF.Sin, scale=CST[:, 1:2]), TABB + [CONSTB], [CSB[i]] + TABB)
                else:
                    K.op("act", lambda e: e.activation(CTAB[i][:, :], FR[:, :], AF.Sin, scale=TWO_PI_SAFE), TABB, [CSB[i]] + TABB)

        def tile_layer(t, l, pre_done=False, mid_hook=None, early_stats=None, early_h=None):
            first = (t == 0)
            X, XB = XT[t % 2], XTB[t % 2]
            ti = t % 2
            v0, d0, m0 = l * VL, l * 32, l * 48

            if not pre_done:
                if l == 0:
                    prenorm_stats(X, XB)
                prenorm_h(X, XB, d0 + 0, m0 + 0)
            qspecs = []
            for g in range(4):
                bk, bkb = bank()
                unit, unitb = W(t, l, g)
                qspecs.append((bk, bkb, unit, unitb))
            proj_multi(qspecs)
            kb, kbb = bank()
            unit, unitb = W(t, l, U_K)
            proj_group(kb, kbb, unit, unitb)
            qk_banks = [(s[0], s[1]) for s in qspecs] + [(kb, kbb)]
            for g in range(5):
                bk, bkb = qk_banks[g]
                bcol = v0 + 32 + g
                K.op("act", lambda e, bk=bk, g=g, bcol=bcol: e.activation(QB[g][:, :], bk[:, :], AF.Identity, bias=VEC[:, bcol:bcol + 1], scale=1.0),
                     [bkb, CONSTB], [QBB[g]])
            bk, bkb = bank()
            unit, unitb = W(t, l, U_V)
            for blk in range(4):
                for kc in range(8):
                    mm(bk[:, blk * 128:(blk + 1) * 128], H[:, kc * T + blk * 128: kc * T + (blk + 1) * 128], unit[:, kc * 128:(kc + 1) * 128],
                       start=(kc == 0), stop=(kc == 7), reads=[unitb, HB[kc]], writes=[bkb], signal=(kc == 7 and blk == 3))
            next_unit()
            K.op("dve", lambda e, bk=bk: e.tensor_tensor(VT[l][:, 128:640], bk[:, :], BVR[:, l * 512:(l + 1) * 512], ALU.add), [bkb, CONSTB], [VTB[l]])
            for g in range(5):
                i = g % 2
                b2, b2b = bank()
                mm(b2[:, :], PERM, QB[g][:, :], start=True, stop=True, reads=[QBB[g], CONSTB], writes=[b2b])
                K.op("dve", lambda e, b2=b2, i=i: e.tensor_tensor(TR[i][:, :], b2[:, :], STAB[ti][:, :], ALU.mult), [b2b, CSB[ti]], [TRB[i]])
                K.op("dve", lambda e, i=i, g=g: e.tensor_tensor(QC[i][:, :], QB[g][:, :], CTAB[ti][:, :], ALU.mult), [QBB[g], CSB[ti]], [QCB[i]])
                if g < 4:
                    K.op("dve", lambda e, i=i, g=g: e.tensor_tensor(xc(QT, g), QC[i][:, :], TR[i][:, :], ALU.add), [QCB[i], TRB[i]], [QTB[g]])
                else:
                    K.op("dve", lambda e, i=i: e.tensor_tensor(KT[l][0:64, 128:640], QC[i][0:64, :], TR[i][0:64, :], ALU.add),
                         [QCB[i], TRB[i]], [KTB[l]])
                    K.op("dve", lambda e, i=i: e.tensor_tensor(KT[l][64:128, 640 + 128:640 + 640], QC[i][64:128, :], TR[i][64:128, :], ALU.add),
                         [QCB[i], TRB[i]], [KTB[l]])
            for gi in range(4):
                bk, bkb = bank()
                unit, unitb = W(t, l, U_U0 + gi)
                proj_group(bk, bkb, unit, unitb)
                bcol = v0 + 37 + gi
                K.op("act", lambda e, bk=bk, gi=gi, bcol=bcol: e.activation(UT[:, gi * 528 + 16: gi * 528 + 528], bk[:, :], AF.Identity,
                                                                               bias=VEC[:, bcol:bcol + 1], scale=1.0), [bkb, CONSTB], [UTB[gi]])
            UT3 = UT[:, :].rearrange("p (g w) -> p g w", g=4)
            K.op("pool", lambda e: e.tensor_copy(UT3[:, :, 0:16], UH[l][:, :].rearrange("p (g w) -> p g w", g=4)), [UHB[l]], UTB)
            for gi in range(4):
                w = 2 << gi
                cur, curb = UT[:, gi * 528:(gi + 1) * 528], UTB[gi]
                for k in range(1, gi + 2):
                    sh = 1 << (k - 1)
                    dst, dstb = LV[k % 2], LVB[k % 2]
                    K.op("pool", lambda e, cur=cur, dst=dst, sh=sh: e.tensor_tensor(dst[:, sh:528], cur[:, sh:528], cur[:, 0:528 - sh], ALU.add),
                         [curb], [dstb])
                    cur, curb = dst[:, :], dstb
                oth, othb = LV[(gi + 2) % 2], LVB[(gi + 2) % 2]
                K.op("pool", lambda e, cur=cur, oth=oth, w=w: e.tensor_scalar(oth[:, 16:528], cur[:, 16:528], 1.0 / w, 0.0, ALU.mult, ALU.add),
                     [curb], [othb])
                K.op("pool", lambda e, oth=oth, gi=gi: e.tensor_tensor(xc(PL, gi), oth[:, 16:528], UT[:, gi * 528 + 16:gi * 528 + 528], ALU.subtract),
                     [othb, UTB[gi]], [PLB[gi]])
                if first:
                    K.op("pool", lambda e, cur=cur, gi=gi: e.tensor_tensor(T16[:, :], cur[:, 16:32], INVC[:, gi * 16:(gi + 1) * 16], ALU.mult),
                         [curb, CONSTB], [T16B])
                    K.op("pool", lambda e, gi=gi: e.tensor_tensor(PL[:, gi * T:gi * T + 16], T16[:, :], UT[:, gi * 528 + 16:gi * 528 + 32], ALU.subtract),
                         [T16B, UTB[gi]], [PLB[gi]])
            K.op("pool", lambda e: e.tensor_copy(UH[l][:, :].rearrange("p (g w) -> p g w", g=4), UT3[:, :, 512:528]), UTB, [UHB[l]])

            pr_i = [0]

            def scores(j):
                for kv in range(2):
                    for pc in range(2):
                        if first and j == 0 and pc == 0:
                            continue
                        bk, bkb = bank()
                        kcol = kv * 640 + (j + pc) * 128
                        for g in range(4):
                            mm(bk[:, g * 128:(g + 1) * 128], KT[l][:, kcol:kcol + 128], QT[:, g * T + j * 128: g * T + (j + 1) * 128],
                               start=True, stop=True, reads=[KTB[l], QTB[g]], writes=[bkb], signal=(g == 3))
                        r = pr_i[0] % 2
                        pr_i[0] += 1
                        K.op("act", lambda e, bk=bk, r=r: e.activation(PR[r][:, :], bk[:, :], AF.Exp, scale=0.125), [bkb], [PRB[r]])
                        pm = (j % 2) * 4 + kv * 2 + pc
                        moff = ((l * 2 + kv) * 2 + pc) * T
                        K.op("dve", lambda e, r=r, pm=pm, moff=moff: e.tensor_tensor(PM[pm][:, :], PR[r][:, :], MK[:, moff:moff + T], ALU.mult),
                             [PRB[r]] + BARRIER, [PMB[pm]])

            def pv(j):
                pcs = [1] if (first and j == 0) else [0, 1]
                ob, obb = bank()
                db, dbb = bank()
                for pc in pcs:
                    for kv in range(2):
                        pm = (j % 2) * 4 + kv * 2 + pc
                        last = (pc == pcs[-1] and kv == 1)
                        mm(db[kv * 64:(kv + 1) * 64, :], ONES1, PM[pm][:, :], start=(pc == pcs[0]), stop=(pc == pcs[-1]),
                           reads=[CONSTB, PMB[pm]], writes=[dbb], signal=last, tp=(0, kv * 64))
                for pc in pcs:
                    for kv in range(2):
                        pm = (j % 2) * 4 + kv * 2 + pc
                        vcol = (j + pc) * 128 + kv * 64
                        last = (pc == pcs[-1] and kv == 1)
                        mm(ob[kv * 64:(kv + 1) * 64, :], VT[l][:, vcol:vcol + 64], PM[pm][:, :], start=(pc == pcs[0]), stop=(pc == pcs[-1]),
                           reads=[VTB[l], PMB[pm]], writes=[obb], signal=last, tp=(0, kv * 64))
                K.op("act", lambda e, db=db: e.activation(RT[:, :], db[:, :], AF.Ln, bias=CST[:, 4:5], scale=1.0), [dbb, CONSTB], [RTB])
                K.op("act", lambda e: e.activation(RT[:, :], RT[:, :], AF.Exp, scale=-1.0), [RTB], [RTB])
                AT3 = AT[:, :].rearrange("p (g t) -> p g t", g=4)
                K.op("dve", lambda e, ob=ob, j=j: e.tensor_tensor(AT3[:, :, j * 128:(j + 1) * 128],
                                                                   ob[:, :].rearrange("p (g q) -> p g q", g=4),
                                                                   RT[:, :].rearrange("p (g q) -> p g q", g=4), ALU.mult), [obb, RTB], ATB)

            scores(0)
            for j in range(4):
                if j + 1 < 4:
                    scores(j + 1)
                pv(j)
            punit, punitb = W(t, l, U_POOL)
            for gi in range(4):
                bk, bkb = bank()
                mm(bk[:, :], punit[:, gi * 128:(gi + 1) * 128], xc(PL, gi), start=True, stop=True, reads=[punitb, PLB[gi]], writes=[bkb])
                if gi == 3:
                    next_unit()
                scol = v0 + 41 + gi
                K.op("act", lambda e, bk=bk, gi=gi, scol=scol: e.activation(xc(PO, gi), bk[:, :], AF.Identity, scale=VEC[:, scol:scol + 1]),
                     [bkb, CONSTB], [POB[gi]])
            KT3 = KT[l][:, :].rearrange("p (m w) -> p m w", m=2)
            K.op("pool", lambda e: e.tensor_copy(KT3[:, :, 0:128], KT3[:, :, 512:640]), [KTB[l]], [KTB[l]])
            K.op("pool", lambda e: e.tensor_copy(VT[l][:, 0:128], VT[l][:, 512:640]), [VTB[l]], [VTB[l]])

            for o in range(8):
                bk, bkb = bank()
                unit, unitb = W(t, l, U_WO0 + o)
                for kc in range(8):
                    rhs, rb = (xc(AT, kc), ATB[kc]) if kc < 4 else (xc(PO, kc - 4), POB[kc - 4])
                    mm(bk[:, :], unit[:, kc * 128:(kc + 1) * 128], rhs, start=(kc == 0), stop=(kc == 7), reads=[unitb, rb], writes=[bkb])
                next_unit()
                K.op("act", lambda e, bk=bk, o=o: e.activation(xc(MIXF, o), bk[:, :], AF.Identity), [bkb], [MIXB[o]])
                sq_from(bk[:, :], bkb, o)
                if o >= 1:
                    sumsq_mm(o - 1)
            sumsq_mm(7)
            postnorm_residual(X, XB, d0 + 8, next_sq=True)

            prenorm_h(X, XB, d0 + 16, m0 + 24)
            K.op("act", lambda e: e.activation(DUM[:, 0:1], CST[:, 4:5], AF.Silu), [CONSTB], [DUMB])

            def ffn_evac(f, bg, bgb, bu, bub):
                i = f % 2
                K.op("act", lambda e: e.activation(SG[i][:, :], bg[:, :], AF.Silu), [bgb], [SGB[i]])
                K.op("dve", lambda e: e.tensor_tensor(xc(ACTB, f), bu[:, :], SG[i][:, :], ALU.mult), [bub, SGB[i]], [ACTBB[f]])

            NF0 = 2
            specs = []
            for f in range(NF0):
                for k2 in range(2):
                    bk, bkb = bank()
                    unit, unitb = W(t, l, U_G0 + 2 * f + k2)
                    specs.append((bk, bkb, unit, unitb))
            proj_multi(specs)
            for f in range(NF0):
                ffn_evac(f, specs[2 * f][0], specs[2 * f][1], specs[2 * f + 1][0], specs[2 * f + 1][1])
            for f in range(NF0, NF):
                bg, bgb = bank()
                unit, unitb = W(t, l, U_G0 + 2 * f)
                proj_group(bg, bgb, unit, unitb)
                bu, bub = bank()
                unit, unitb = W(t, l, U_G0 + 2 * f + 1)
                proj_group(bu, bub, unit, unitb)
                ffn_evac(f, bg, bgb, bu, bub)
                if f == 10 and mid_hook is not None:
                    mid_hook()
            K.op("act", lambda e: e.activation(DUM[:, 1:2], CST[:, 4:5], AF.Ln), [CONSTB], [DUMB])
            for o in range(8):
                bk, bkb = bank()
                s = wslot_of[(t, l, o)]
                for f in range(NF):
                    mm(bk[:, :], WDR[s][:, f * 128:(f + 1) * 128], xc(ACTB, f), start=(f == 0), stop=(f == NF - 1),
                       reads=[WDRB[s], ACTBB[f]], writes=[bkb])
                next_wd()
                K.op("act", lambda e, bk=bk, o=o: e.activation(xc(MIXF, o), bk[:, :], AF.Identity), [bkb], [MIXB[o]])
                sq_from(bk[:, :], bkb, o)
                if o >= 1:
                    sumsq_mm(o - 1)
                if o == 3 and early_stats is not None:
                    early_stats()
            sumsq_mm(7)
            if early_h is not None:
                early_h()
            postnorm_residual(X, XB, d0 + 24, next_sq=(l + 1 < L))

        dma("sp", XT[0][:, :], xT_d[0, :, :], [], XTB[0], d_x[0])
        for _ in range(NSLOT):
            next_unit()
        for _ in range(NWD):
            next_wd()
        make_tables(0)
        if NT > 1:
            dma("sp", XT[1][:, :], xT_d[1, :, :], [], XTB[1], d_x[1])
        PMS = (PM[0:4], PMB[0:4])
        for t in range(NT):
            nxt = t + 1 < NT
            Xn, XnB = XT[(t + 1) % 2], XTB[(t + 1) % 2]

            def mid_hook(t=t):
                if t + 1 < NT:
                    if t >= 1:
                        i = (t + 1) % 2
                        dma("sp", XT[i][:, :], xT_d[t + 1, :, :], [], XTB[i], d_x[i])
                    make_tables(t + 1)

            def early_stats(Xn=Xn, XnB=XnB):
                prenorm_stats(Xn, XnB, NRM2, NRM2B, PMS)

            def early_h(Xn=Xn, XnB=XnB):
                prenorm_h(Xn, XnB, 0, 0, NRM2, NRM2B, RT, RTB)

            for l in range(L):
                last = (l == L - 1)
                tile_layer(t, l, pre_done=(l == 0 and t > 0), mid_hook=(mid_hook if l == 0 else None),
                           early_stats=(early_stats if (last and nxt) else None), early_h=(early_h if (last and nxt) else None))
            dma("pool", out_d[t, :, :], XT[t % 2][:, :], XTB[t % 2], [OUTB], d_out)
        K.streams["pool"].ops.append(lambda e: e.wait_ge(d_out.h, d_out.count))
        if debug:
            K.streams["act"].ops.append(lambda e: e.wait_ge(d_dbg.h, d_dbg.count))

        with nc.Block() as block:
            @block.sync
            def _(e):
                for f in K.streams["sp"].ops:
                    f(e)

            @block.tensor
            def _(e):
                for f in K.streams["pe"].ops:
                    f(e)

            @block.scalar
            def _(e):
                for f in K.streams["act"].ops:
                    f(e)

            @block.vector
            def _(e):
                for f in K.streams["dve"].ops:
                    f(e)

            @block.gpsimd
            def _(e):
                for f in K.streams["pool"].ops:
                    f(e)
    return nc


def _unit(Wm, rows, cols):
    sub = Wm[np.asarray(rows)][:, np.asarray(cols)]
    return sub.reshape(8, 128, 128).transpose(1, 0, 2)


def pack_shared(inp, L=2):
    f32 = np.float32
    ada_w, ada_b = np.asarray(inp["ada_w"], f32), np.asarray(inp["ada_b"], f32)
    w_in, b_in = np.asarray(inp["w_in"], f32), np.asarray(inp["b_in"], f32)
    sinks = np.asarray(inp["sinks"], f32)
    pool_w, pool_scale = np.asarray(inp["pool_w"], f32), np.asarray(inp["pool_scale"], f32)
    w_out = np.asarray(inp["w_out"], f32)
    w_gate, w_up, w_down = np.asarray(inp["w_gate"], f32), np.asarray(inp["w_up"], f32), np.asarray(inp["w_down"], f32)
    gs = [np.asarray(inp[k], f32) for k in ("g_pre_mix", "g_post_mix", "g_pre_ffn", "g_post_ffn")]

    adaw = np.empty((L * 24, 128, 2, 8, 128), f32)
    for l in range(L):
        adaw[l * 24:(l + 1) * 24] = ada_w[l].reshape(8, 128, 24, 2, 128).transpose(2, 1, 3, 0, 4)
    adaw = adaw.reshape(L * 24, 128, 2048)
    adab = np.concatenate([ada_b[l].reshape(48, 128).T for l in range(L)], axis=1)

    nat = np.arange(1024)
    qcols = [np.concatenate([np.arange(g * 64, g * 64 + 64), np.arange((g + 4) * 64, (g + 4) * 64 + 64)]) for g in range(4)]
    mixrows = np.concatenate([qcols[g] for g in range(4)] + [np.arange(512, 1024)])
    wu = np.zeros((L, NU, 128, 8, 128), f32)
    wd = np.empty((L, 8, 128, NF, 128), f32)
    for l in range(L):
        for g in range(4):
            wu[l, g] = _unit(w_in[l], nat, qcols[g])
        wu[l, U_K] = _unit(w_in[l], nat, np.arange(512, 640))
        wu[l, U_V] = _unit(w_in[l], nat, np.arange(640, 768))
        for gi in range(4):
            wu[l, U_U0 + gi] = _unit(w_in[l], nat, np.arange(768 + gi * 128, 768 + (gi + 1) * 128))
            wu[l, U_POOL, :, gi, :] = pool_w[l, gi]
        for o in range(8):
            wu[l, U_WO0 + o] = _unit(w_out[l], mixrows, np.arange(o * 128, (o + 1) * 128))
            wd[l, o] = w_down[l][:, o * 128:(o + 1) * 128].reshape(NF, 128, 128).transpose(1, 0, 2)
        for f in range(NF):
            wu[l, U_G0 + 2 * f] = _unit(w_gate[l], nat, np.arange(f * 128, (f + 1) * 128))
            wu[l, U_G0 + 2 * f + 1] = _unit(w_up[l], nat, np.arange(f * 128, (f + 1) * 128))
    wu = wu.reshape(-1, 16384)
    wd = wd.reshape(-1, 11264)

    def pcol(v):
        return v.reshape(-1, 128).T

    vecs = np.empty((128, L * VL), f32)
    bvrep = np.empty((128, L * 512), f32)
    sinkrep = np.empty((128, L * 8), f32)
    for l in range(L):
        v0 = l * VL
        for i in range(4):
            vecs[:, v0 + 8 * i:v0 + 8 * i + 8] = pcol(gs[i][l])
        for g in range(4):
            vecs[:, v0 + 32 + g] = b_in[l][qcols[g]]
        vecs[:, v0 + 36] = b_in[l][512:640]
        vecs[:, v0 + 37:v0 + 41] = pcol(b_in[l][768:1280])
        vecs[:, v0 + 41:v0 + 45] = pcol(pool_scale[l])
        bvrep[:, l * 512:(l + 1) * 512] = np.tile(b_in[l][640:768], (128, 4))
        sinkrep[:, l * 8:(l + 1) * 8] = np.tile(sinks[l], (128, 1))

    cst = np.zeros((128, 8), f32)
    inv_freq = (ROPE_THETA ** (-np.arange(0, 16, 2, dtype=np.float32) / 16)).astype(f32)
    for p in range(128):
        r = p % 64
        if r < 16:
            cst[p, 0] = np.float32(inv_freq[r % 8]) / np.float32(2 * np.pi)
            cst[p, 1] = (-1.0 if r < 8 else 1.0) * TWO_PI_SAFE
    cst[:, 2] = TWO_PI_SAFE
    cst[:, 3] = EPS
    cst[:, 4] = 1.0
    invc = np.zeros((128, 64), f32)
    for gi in range(4):
        w = 2 << gi
        invc[:, gi * 16:(gi + 1) * 16] = 1.0 / np.minimum(np.arange(16) + 1.0, float(w))
    cmat = np.zeros((128, 576), f32)
    cmat[:, 0:128] = 1.0 / 1024.0
    cmat[:, 128:192] = 1.0
    for m in range(128):
        r = m % 64
        if r < 8:
            cmat[m + 8, 192 + m] = 1.0
        elif r < 16:
            cmat[m - 8, 192 + m] = 1.0
    s_idx = np.arange(128)[:, None]
    q_idx = np.arange(128)[None, :]
    cmat[:, 320:448] = (s_idx > q_idx).astype(f32)
    cmat[:, 448:576] = (s_idx <= q_idx).astype(f32)
    return dict(adaw=adaw, adab=np.ascontiguousarray(adab), wu=wu, wd=wd, vecs=vecs, bvrep=bvrep, sinkrep=sinkrep,
                cst=cst, invc=invc, cmat=cmat)


def pack_core(x_b, c_b, pos_b):
    S = x_b.shape[0]
    NT = S // T
    xT = np.ascontiguousarray(np.asarray(x_b, np.float32).reshape(NT, T, 8, 128).transpose(0, 3, 2, 1)).reshape(NT, 128, 8 * T)
    pos = np.ascontiguousarray(np.broadcast_to(np.asarray(pos_b, np.int32)[None, :], (128, S)))
    cT = np.ascontiguousarray(np.asarray(c_b, np.float32).reshape(8, 128).T)
    return dict(xT=xT, pos=pos, cT=cT)


def unpack_out(outT, S):
    NT = S // T
    return outT.reshape(NT, 128, 8, T).transpose(0, 3, 2, 1).reshape(S, D)


_CACHE = {}


def kernel(_debug=False, **inputs):
    x = np.asarray(inputs["x"])
    B, S, _ = x.shape
    shared = pack_shared(inputs)
    in_maps = []
    for b in range(B):
        m = dict(shared)
        m.update(pack_core(x[b], np.asarray(inputs["c"])[b], np.asarray(inputs["positions"])[b]))
        in_maps.append(m)
    key = (S, _debug)
    if key not in _CACHE:
        _CACHE[key] = build_program(S, debug=_debug)
    nc = _CACHE[key]
    res = run_bass_kernel_spmd(nc, in_maps, core_ids=list(range(B)))
    if _debug:
        kernel.dbg = [np.where(np.arange(24)[:, None, None] >= 0, 0, 0) for b in range(0)]
        for b in range(B):
            d32 = np.asarray(res.results[b]["dbg"]).astype(np.float32)
            d16 = np.asarray(res.results[b]["dbgh"]).astype(np.float32)
            halfidx = [2, 3, 4, 5, 6, 8, 9, 15, 16, 10, 13]
            for i in halfidx:
                d32[i] = d16[i]
            kernel.dbg.append(d32)
    out = np.empty((B, S, D), np.float32)
    for b in range(B):
        out[b] = unpack_out(np.asarray(res.results[b]["outT"]), S)
    return out
```
